# Optimizing a Trainium2 kernel written in Bass

```python
import jax, jax.numpy as jnp
from jax import lax
import numpy as np

D_MODEL = 4096
BATCH = 8
SEQ = 2048
DEPTH = 4

CTX_LEN = 256
GRID_W = 64
N_MOD = 9
MOD_RANK = 256
D_FF = 3072
CONV_W = 1024
CONV_K = 3
GLA_HEADS = 4
GLA_DK = 192
GLA_DV = 384
GLA_DK_T = GLA_HEADS * GLA_DK
GLA_DV_T = GLA_HEADS * GLA_DV
GLA_DECAY_RANK = 16
GLA_TAU = 16.0
GLA_CHUNK = 64
NA_HEADS = 12
NA_HD = 128
NA_W = NA_HEADS * NA_HD
NA_KR = 8
NA_KC = 16
ROPE_THETA = 10000.0
EPS = 1e-6
IN_SIZES = (CONV_W, CONV_W, CONV_W,
            GLA_DK_T, GLA_DK_T, GLA_DV_T, GLA_DV_T, 2 * GLA_DECAY_RANK,
            NA_W, NA_W, NA_W,
            D_MODEL, D_MODEL, D_MODEL)
N_IN = sum(IN_SIZES)

kernel_name = 'hybrid_gated_branch_flow_block'


def split_cols(p):
    offs = [int(o) for o in np.cumsum(IN_SIZES)[:-1]]
    return jnp.split(p, offs, axis=-1)


def rmsnorm(x, g):
    xf = x.astype(jnp.float32)
    y = xf * lax.rsqrt(jnp.mean(xf * xf, axis=-1, keepdims=True) + EPS)
    return (y * g.astype(jnp.float32)).astype(x.dtype)


def sub_in(t, m, i, g):
    return rmsnorm(t, g) * (1 + m[..., 3 * i + 1, :, :]) + m[..., 3 * i, :, :]


def gate_of(m, i):
    return m[..., 3 * i + 2, :, :]


def swiglu(u, w_up, w_down):
    a, b = jnp.split(u @ w_up, 2, axis=-1)
    return (jax.nn.silu(a) * b) @ w_down


def short_conv(u, w):
    return lax.conv_general_dilated(u, w[:, None, :], window_strides=(1,),
                                    padding=[(CONV_K // 2, CONV_K // 2)],
                                    dimension_numbers=('NWC', 'WIO', 'NWC'),
                                    feature_group_count=u.shape[-1])


def conv_branch(h, bg, cg, w, wb):
    return (bg * short_conv(cg * h, w)) @ wb


def split_heads(t, nh):
    return t.reshape(t.shape[0], t.shape[1], nh, -1)


def _flip(t):
    return jnp.flip(t, axis=2)


def _rotate(u, pos):
    n = u.shape[-1] // 2
    freq = ROPE_THETA ** (-jnp.arange(n, dtype=jnp.float32) / n)
    ang = pos.astype(jnp.float32)[:, None] * freq[None, :]
    cos = jnp.cos(ang)[None, :, None, :]
    sin = jnp.sin(ang)[None, :, None, :]
    u1 = u[..., :n].astype(jnp.float32)
    u2 = u[..., n:].astype(jnp.float32)
    return jnp.concatenate([u1 * cos - u2 * sin, u1 * sin + u2 * cos], axis=-1)


def axial_rope(t):
    pos = jnp.arange(t.shape[1])
    half = t.shape[-1] // 2
    out = jnp.concatenate([_rotate(t[..., :half], pos // GRID_W),
                           _rotate(t[..., half:], pos % GRID_W)], axis=-1)
    return out.astype(t.dtype)


def gla_log_decay(lr, w, b):
    bsz, seq, _ = lr.shape
    la = jax.nn.log_sigmoid((lr @ w + b).astype(jnp.float32)) / GLA_TAU
    return la.reshape(bsz, seq, GLA_HEADS, GLA_DK).transpose(0, 2, 1, 3)


def gla_chunked(q, k, v, log_a, s0):
    bsz, nh, seq, dk = q.shape
    dv = v.shape[-1]
    n = seq // GLA_CHUNK
    q, k, log_a = (t.reshape(bsz, nh, n, GLA_CHUNK, dk) for t in (q, k, log_a))
    v = v.reshape(bsz, nh, n, GLA_CHUNK, dv)
    cum = jnp.cumsum(log_a, axis=3)
    last = cum[:, :, :, -1:, :]
    q_dec = q * jnp.exp(cum)
    att = jnp.einsum('bhnid,bhnjd->bhnij', q_dec, k * jnp.exp(-cum))
    tri = jnp.tril(jnp.ones((GLA_CHUNK, GLA_CHUNK), dtype=bool))
    att = jnp.where(tri, att, 0.0)
    o_intra = jnp.einsum('bhnij,bhnjv->bhniv', att, v)
    kv = jnp.einsum('bhnjd,bhnjv->bhndv', k * jnp.exp(last - cum), v)

    def step(s, inp):
        dec, kv_n = inp
        return s * dec[..., None] + kv_n, s

    _, s_start = lax.scan(step, s0, (jnp.moveaxis(jnp.exp(last[:, :, :, 0, :]), 2, 0),
                                     jnp.moveaxis(kv, 2, 0)))
    o_inter = jnp.einsum('bhnid,bhndv->bhniv', q_dec, jnp.moveaxis(s_start, 0, 2))
    return (o_intra + o_inter).reshape(bsz, nh, seq, dv)


def gla_final_state(k, v, log_a):
    cum = jnp.cumsum(log_a, axis=2)
    return jnp.einsum('bhld,bhlv->bhdv', k * jnp.exp(cum[:, :, -1:, :] - cum), v)


def gla_branch(q, k, v, g, la_f, la_b, s_f, s_b, norm_g, wb):
    o = gla_chunked(q, k, v, la_f, s_f) + _flip(gla_chunked(_flip(q), _flip(k), _flip(v), _flip(la_b), s_b))
    o = o.transpose(0, 2, 1, 3)
    o = o * lax.rsqrt(jnp.mean(o * o, axis=-1, keepdims=True) + EPS) * norm_g.reshape(GLA_HEADS, GLA_DV).astype(jnp.float32)
    o = o.reshape(o.shape[0], o.shape[1], GLA_DV_T).astype(g.dtype) * jax.nn.silu(g)
    return o @ wb


def na_latent(q, k, v, kc, vc, rpb):
    bsz, seq, nh, hd = q.shape
    rows = seq // GRID_W
    kr = min(NA_KR, rows)
    ncb = GRID_W // NA_KC
    nloc = kr * 2 * NA_KC
    qg = q.reshape(bsz, rows, ncb, NA_KC, nh, hd)
    kg = k.reshape(bsz, rows, GRID_W, nh, hd)
    vg = v.reshape(bsz, rows, GRID_W, nh, hd)
    qcol = jnp.arange(ncb)[:, None] * NA_KC + jnp.arange(NA_KC)[None, :]
    win_c = jnp.clip(qcol - NA_KC // 2, 0, GRID_W - NA_KC)
    blk_c = jnp.minimum(win_c[:, 0], GRID_W - 2 * NA_KC)
    key_c = blk_c[:, None] + jnp.arange(2 * NA_KC)[None, :]
    col_valid = (key_c[:, None, :] >= win_c[..., None]) & (key_c[:, None, :] < win_c[..., None] + NA_KC)
    col_off = jnp.clip(key_c[:, None, :] - qcol[..., None] + NA_KC - 1, 0, 2 * NA_KC - 2)
    win_r = jnp.clip(jnp.arange(rows) - NA_KR // 2, 0, rows - kr)

    def row_block(args):
        r, rs, qb = args
        kb = lax.dynamic_slice_in_dim(kg, rs, kr, axis=1)[:, :, key_c]
        vb = lax.dynamic_slice_in_dim(vg, rs, kr, axis=1)[:, :, key_c]
        bias = rpb[:, rs + jnp.arange(kr) - r + NA_KR - 1][:, :, col_off]
        bias = bias.transpose(0, 2, 3, 1, 4).astype(jnp.float32)
        s_loc = jnp.einsum('bcihd,brcjhd->bhcirj', qb, kb).astype(jnp.float32) + bias
        s_loc = jnp.where(col_valid[:, :, None, :], s_loc, -jnp.inf)
        s_ctx = jnp.einsum('bcihd,bmhd->bhcim', qb, kc).astype(jnp.float32)
        p = jax.nn.softmax(jnp.concatenate([s_loc.reshape(bsz, nh, ncb, NA_KC, nloc), s_ctx], axis=-1), axis=-1)
        p = p.astype(v.dtype)
        p_loc = p[..., :nloc].reshape(bsz, nh, ncb, NA_KC, kr, 2 * NA_KC)
        o = jnp.einsum('bhcirj,brcjhd->bcihd', p_loc, vb) + jnp.einsum('bhcim,bmhd->bcihd', p[..., nloc:], vc)
        return o.reshape(bsz, GRID_W, nh, hd)

    out = lax.map(row_block, (jnp.arange(rows), win_r, jnp.moveaxis(qg, 1, 0)))
    return jnp.moveaxis(out, 0, 1).reshape(bsz, seq, nh * hd)


def na_context(q, k, v):
    s = jnp.einsum('bmhd,bnhd->bhmn', q, k).astype(jnp.float32)
    p = jax.nn.softmax(s, axis=-1).astype(v.dtype)
    o = jnp.einsum('bhmn,bnhd->bmhd', p, v)
    return o.reshape(o.shape[0], o.shape[1], -1)


def token_mixers(xm, cm, w_in, conv_w, decay_w, decay_b, gla_norm_g, na_rpb,
                 wb_conv, wb_gla, wb_na, w_out, ctx_out):
    (x_h, x_bg, x_cg, x_q, x_k, x_v, x_g, x_lr, x_nq, x_nk, x_nv, x_ga, x_gb, x_gc) = split_cols(xm @ w_in)
    (c_h, c_bg, c_cg, c_q, c_k, c_v, c_g, c_lr, c_nq, c_nk, c_nv, c_ga, c_gb, c_gc) = split_cols(cm @ w_in)
    r = GLA_DECAY_RANK
    q_scale = GLA_DK ** -0.5
    na_scale = NA_HD ** -0.5

    def to_bhld(t, nh):
        return split_heads(t, nh).transpose(0, 2, 1, 3).astype(jnp.float32)

    kc = to_bhld(c_k, GLA_HEADS)
    vc = to_bhld(c_v, GLA_HEADS)
    la_cf = gla_log_decay(c_lr[..., :r], decay_w[0], decay_b[0])
    la_cb = gla_log_decay(c_lr[..., r:], decay_w[1], decay_b[1])
    s_f = gla_final_state(kc, vc, la_cf)
    s_b = gla_final_state(_flip(kc), _flip(vc), _flip(la_cb))
    qx = axial_rope(split_heads(x_q, GLA_HEADS)).transpose(0, 2, 1, 3).astype(jnp.float32) * q_scale
    kx = axial_rope(split_heads(x_k, GLA_HEADS)).transpose(0, 2, 1, 3).astype(jnp.float32)
    vx = to_bhld(x_v, GLA_HEADS)
    la_xf = gla_log_decay(x_lr[..., :r], decay_w[0], decay_b[0])
    la_xb = gla_log_decay(x_lr[..., r:], decay_w[1], decay_b[1])
    y_b = gla_branch(qx, kx, vx, x_g, la_xf, la_xb, s_f, s_b, gla_norm_g, wb_gla)
    nkc = split_heads(c_nk, NA_HEADS)
    nvc = split_heads(c_nv, NA_HEADS)
    y_c = na_latent(split_heads(x_nq, NA_HEADS) * na_scale, split_heads(x_nk, NA_HEADS),
                    split_heads(x_nv, NA_HEADS), nkc, nvc, na_rpb) @ wb_na
    y_a = conv_branch(x_h, x_bg, x_cg, conv_w, wb_conv)
    y_x = (jax.nn.sigmoid(x_ga) * y_a + jax.nn.sigmoid(x_gb) * y_b + jax.nn.sigmoid(x_gc) * y_c) @ w_out
    if not ctx_out:
        return y_x, None
    zeros = jnp.zeros_like(s_f)
    qc = to_bhld(c_q, GLA_HEADS) * q_scale
    yc_b = gla_branch(qc, kc, vc, c_g, la_cf, la_cb, zeros, zeros, gla_norm_g, wb_gla)
    yc_c = na_context(split_heads(c_nq, NA_HEADS) * na_scale, nkc, nvc) @ wb_na
    yc_a = conv_branch(c_h, c_bg, c_cg, conv_w, wb_conv)
    y_cx = (jax.nn.sigmoid(c_ga) * yc_a + jax.nn.sigmoid(c_gb) * yc_b + jax.nn.sigmoid(c_gc) * yc_c) @ w_out
    return y_x, y_cx


def setup_inputs(seed: int = 0) -> dict:
    key = jax.random.key(seed)
    ks = jax.random.split(key, 24)

    def nrm(k, shape, scale):
        return jax.random.normal(k, shape, jnp.float32) * scale

    return {
        'x': nrm(ks[0], (BATCH, SEQ, D_MODEL), 1.0),
        'c': nrm(ks[1], (BATCH, D_MODEL), 1.0),
        'ctx': nrm(ks[2], (BATCH, CTX_LEN, D_MODEL), 1.0),
        'c_ctx': nrm(ks[3], (D_MODEL,), 1.0),
        'mod_a': nrm(ks[4], (DEPTH, D_MODEL, MOD_RANK), D_MODEL ** -0.5),
        'mod_b': nrm(ks[5], (DEPTH, MOD_RANK, N_MOD * D_MODEL), 0.5 * MOD_RANK ** -0.5),
        'mod_bias': nrm(ks[6], (DEPTH, N_MOD * D_MODEL), 0.02),
        'norm_g': 1.0 + nrm(ks[7], (DEPTH, 3, D_MODEL), 0.02),
        'ffn_up': nrm(ks[8], (DEPTH, 2, D_MODEL, 2 * D_FF), D_MODEL ** -0.5),
        'ffn_down': nrm(ks[9], (DEPTH, 2, D_FF, D_MODEL), D_FF ** -0.5),
        'w_in': nrm(ks[10], (DEPTH, D_MODEL, N_IN), D_MODEL ** -0.5),
        'conv_w': nrm(ks[11], (DEPTH, CONV_K, CONV_W), CONV_K ** -0.5),
        'gla_decay_w': nrm(ks[12], (DEPTH, 2, GLA_DECAY_RANK, GLA_DK_T), GLA_DECAY_RANK ** -0.5),
        'gla_decay_b': nrm(ks[13], (DEPTH, 2, GLA_DK_T), 0.1),
        'gla_norm_g': 1.0 + nrm(ks[14], (DEPTH, GLA_DV_T), 0.02),
        'na_rpb': nrm(ks[15], (DEPTH, NA_HEADS, 2 * NA_KR - 1, 2 * NA_KC - 1), 0.1),
        'w_branch_conv': nrm(ks[16], (DEPTH, CONV_W, D_MODEL), CONV_W ** -0.5),
        'w_branch_gla': nrm(ks[17], (DEPTH, GLA_DV_T, D_MODEL), GLA_DV_T ** -0.5),
        'w_branch_na': nrm(ks[18], (DEPTH, NA_W, D_MODEL), NA_W ** -0.5),
        'w_out': nrm(ks[19], (DEPTH, D_MODEL, D_MODEL), D_MODEL ** -0.5),
        'final_g': 1.0 + nrm(ks[20], (D_MODEL,), 0.02),
    }


def reference(x, c, ctx, c_ctx, mod_a, mod_b, mod_bias, norm_g, ffn_up, ffn_down, w_in, conv_w,
              gla_decay_w, gla_decay_b, gla_norm_g, na_rpb, w_branch_conv, w_branch_gla, w_branch_na,
              w_out, final_g):
    h, hc = x, ctx
    sc = jax.nn.silu(c)
    scc = jax.nn.silu(c_ctx)
    for layer in range(DEPTH):
        last = layer == DEPTH - 1
        m_x = ((sc @ mod_a[layer]) @ mod_b[layer] + mod_bias[layer]).reshape(c.shape[0], N_MOD, 1, D_MODEL)
        m_c = ((scc @ mod_a[layer]) @ mod_b[layer] + mod_bias[layer]).reshape(N_MOD, 1, D_MODEL)
        up0, dn0 = ffn_up[layer, 0], ffn_down[layer, 0]
        h = h + 0.5 * gate_of(m_x, 0) * swiglu(sub_in(h, m_x, 0, norm_g[layer, 0]), up0, dn0)
        hc = hc + 0.5 * gate_of(m_c, 0) * swiglu(sub_in(hc, m_c, 0, norm_g[layer, 0]), up0, dn0)
        y_x, y_c = token_mixers(sub_in(h, m_x, 1, norm_g[layer, 1]), sub_in(hc, m_c, 1, norm_g[layer, 1]),
                                w_in[layer], conv_w[layer], gla_decay_w[layer], gla_decay_b[layer],
                                gla_norm_g[layer], na_rpb[layer], w_branch_conv[layer], w_branch_gla[layer],
                                w_branch_na[layer], w_out[layer], not last)
        h = h + gate_of(m_x, 1) * y_x
        up1, dn1 = ffn_up[layer, 1], ffn_down[layer, 1]
        h = h + 0.5 * gate_of(m_x, 2) * swiglu(sub_in(h, m_x, 2, norm_g[layer, 2]), up1, dn1)
        if not last:
            hc = hc + gate_of(m_c, 1) * y_c
            hc = hc + 0.5 * gate_of(m_c, 2) * swiglu(sub_in(hc, m_c, 2, norm_g[layer, 2]), up1, dn1)
    return rmsnorm(h, final_g)
```

```python
import contextlib
import math
import numpy as np
import concourse.bass as bass
import concourse.mybir as mybir
from concourse.bass_utils import run_bass_kernel_spmd

F32 = mybir.dt.float32
BF16 = mybir.dt.bfloat16
AF = mybir.ActivationFunctionType
ALU = mybir.AluOpType
AX = mybir.AxisListType

EPS = 1e-6


class Region:
    __slots__ = ("name", "last_w", "readers")

    def __init__(self, name=""):
        self.name = name
        self.last_w = None
        self.readers = []


class Tile:
    def __init__(self, t, nreg=1, name=""):
        self.t = t
        self.regs = [Region(f"{name}.{i}") for i in range(nreg)]

    def __getitem__(self, k):
        return self.t[k]

    @property
    def all(self):
        return self.regs

    def r(self, i):
        return [self.regs[i]]


class Eng:
    def __init__(self, P, name, e):
        self.P = P
        self.name = name
        self.e = e
        self.sem = None
        self.count = 0
        self.waited = {}
        self.pend_r = []
        self.pend_w = []

    def new_sem(self):
        self.sem = self.P.alloc_sem(f"s_{self.name}_{self.P.nsem}")
        self.count = 0


SEM_ROLL = 30000


class Prog:
    def __init__(self, nc):
        self.nc = nc
        self.es = contextlib.ExitStack()
        self.nsem = 0
        self.pe = Eng(self, "pe", nc.tensor)
        self.act = Eng(self, "act", nc.scalar)
        self.dve = Eng(self, "dve", nc.vector)
        self.pool = Eng(self, "pool", nc.gpsimd)
        self.sp = Eng(self, "sp", nc.sync)
        self.engs = [self.pe, self.act, self.dve, self.pool, self.sp]
        self.all_sems = []
        for E in self.engs:
            E.new_sem()
        self.dsems = [[self.alloc_sem(f"dma{i}"), 0] for i in range(24)]
        self.dnext = 0
        self.qsems = [[self.alloc_sem(f"swdma{i}"), 0] for i in range(8)]
        self.qnext = 0
        self.n_ins = 0
        self.ps_tiles = None
        self.ps_next = 0
        self.ev_flip = 0

    def alloc_sem(self, name):
        self.nsem += 1
        return self.es.enter_context(self.nc.semaphore(name))

    def sbuf(self, name, shape, dt, nreg=1, stack=None):
        self.uid = getattr(self, "uid", 0) + 1
        name = f"sb{self.uid}_{name}"
        t = (stack or self.es).enter_context(self.nc.sbuf_tensor(name, list(shape), dt))
        return Tile(t, nreg, name)

    def psum(self, name, shape, dt, nreg=1):
        t = self.es.enter_context(self.nc.psum_tensor(name, list(shape), dt))
        return Tile(t, nreg, name)

    def dram(self, name, shape, dt, kind="Internal", nreg=1):
        t = self.nc.dram_tensor(name, list(shape), dt, kind=kind)
        return Tile(t, nreg, name)

    def next_ps(self):
        n = getattr(self, "ps_rot", 8)
        self.ps_next = (self.ps_next + 1) % n
        return self.ps_tiles[self.ps_next]

    def _waits(self, E, r, w):
        need = {}

        def add(ev):
            if ev is None:
                return
            s, v = ev
            k = id(s)
            if k not in need or need[k][1] < v:
                need[k] = (s, v)

        for reg in r:
            add(reg.last_w)
        for reg in w:
            add(reg.last_w)
            for ev in reg.readers:
                add(ev)
        for k, (s, v) in need.items():
            if E.waited.get(k, 0) >= v:
                continue
            if s is E.sem and v > E.count:
                continue
            E.e.wait_ge(s, v)
            E.waited[k] = v

    def _commit(self, ev, r, w):
        for reg in r:
            reg.readers.append(ev)
            if len(reg.readers) > 64:
                best = {}
                for s, v in reg.readers:
                    if id(s) not in best or best[id(s)][1] < v:
                        best[id(s)] = (s, v)
                reg.readers = list(best.values())
        for reg in w:
            reg.last_w = ev
            reg.readers = []

    def op(self, E, fn, r=(), w=(), sig=True):
        r = list(r)
        w = list(w)
        if E.count >= SEM_ROLL and not E.pend_r and not E.pend_w:
            E.new_sem()
        self._waits(E, r, w)
        ins = fn(E.e)
        self.n_ins += 1
        if sig:
            E.count += 1
            ins.then_inc(E.sem, 1)
            ev = (E.sem, E.count)
            self._commit(ev, r + E.pend_r, w + E.pend_w)
            E.pend_r = []
            E.pend_w = []
        else:
            E.pend_r += r
            E.pend_w += w
            self._commit((E.sem, E.count + 1), r, w)
        return ins

    def dma(self, E, out, in_, r=(), w=(), **kw):
        r = list(r)
        w = list(w)
        if E is self.pool:
            slot = self.qsems[self.qnext]
            self.qnext = (self.qnext + 1) % len(self.qsems)
        else:
            slot = self.dsems[self.dnext]
            self.dnext = (self.dnext + 1) % len(self.dsems)
        s, tot = slot
        k = id(s)
        if tot > 0 and E.waited.get(k, 0) < tot:
            E.e.wait_ge(s, tot)
            E.waited[k] = tot
        self._waits(E, r, w)
        ins = E.e.dma_start(out=out, in_=in_, **kw)
        ins.then_inc(s, 16)
        slot[1] = tot + 16
        self.n_ins += 1
        self._commit((s, tot + 16), r, w)

    def barrier(self):
        for X in self.engs:
            assert not X.pend_r and not X.pend_w
        for E in self.engs:
            for X in self.engs:
                if X is E or X.count == 0:
                    continue
                k = id(X.sem)
                if E.waited.get(k, 0) < X.count:
                    E.e.wait_ge(X.sem, X.count)
                    E.waited[k] = X.count
            for s, tot in self.dsems + self.qsems:
                if tot > 0 and E.waited.get(id(s), 0) < tot:
                    E.e.wait_ge(s, tot)
                    E.waited[id(s)] = tot

    def ev_eng(self):
        self.ev_flip ^= 1
        return self.act if self.ev_flip else self.dve


class Cfg:
    def __init__(self, D=4096, FF=3072, CW=1024, ROWS=32, M=256, DEPTH=4, RANK=256):
        self.D, self.FF, self.CW, self.ROWS, self.M, self.DEPTH, self.RANK = D, FF, CW, ROWS, M, DEPTH, RANK
        self.GW = 64
        self.L = ROWS * 64
        self.T = self.M + self.L
        self.GH, self.DK, self.DV = 4, 192, 384
        self.NH, self.HD = 12, 128
        self.KR, self.KC = 8, 16
        self.KCD = D // 128
        self.GK = self.GH * self.DK
        self.GV = self.GH * self.DV
        self.NW = self.NH * self.HD
        sizes = (CW, CW, CW, self.GK, self.GK, self.GV, self.GV, 32, self.NW, self.NW, self.NW, D, D, D)
        offs = np.concatenate([[0], np.cumsum(sizes)])
        names = ["h", "bg", "cg", "q", "k", "v", "g", "lr", "nq", "nk", "nv", "ga", "gb", "gc"]
        self.off = {n: int(o) for n, o in zip(names, offs[:-1])}
        self.NIN = int(offs[-1])
        self.blocks = [(0, self.M, 1)] + [(self.M + i * 512, min(512, self.L - i * 512), 0)
                                          for i in range((self.L + 511) // 512)]


USE_WCACHE = True
WCOLS = 256
WBUF_ELEMS = 32 * 256


class Ctx:
    pass


def w_view(Wt, row0, nrows, col0, ncols):
    ap = Wt[row0:row0 + nrows, col0:col0 + ncols]
    if nrows <= 128:
        return ap
    return ap.rearrange("(kc p) n -> p kc n", p=128)


NSLOT = 112


def load_w(P, C, Wt, wreg, row0, nrows, col0, ncols, dst_col=0, wt=None, perm48=False, key="auto"):
    if wt is None:
        wt = C.wbufs[C.wnext]
        C.wnext = (C.wnext + 1) % len(C.wbufs)
    kc = max(1, nrows // 128)
    pp = min(nrows, 128)
    view = wt.t[:pp, 0:kc * WCOLS].rearrange("p (kc c) -> p kc c", c=WCOLS)
    cache = getattr(C, "wcache", None)
    if key == "auto":
        key = (id(wreg[0]), row0, nrows, col0, ncols, perm48, dst_col)
    slot = None
    if cache is not None and key is not None:
        slot = C.wslots.get(key)
        if slot is not None:
            cv = cache.t[slot, :pp, 0:kc * WCOLS].rearrange("p (kc c) -> p kc c", c=WCOLS)
            P.dma(P.pool, view[:, :, 0:ncols], cv[:, :, 0:ncols], r=cache.r(slot), w=wt.all)
            return wt, view
    if not perm48:
        src = w_view(Wt, row0, nrows, col0, ncols)
        if nrows <= 128:
            P.dma(P.pool, view[:, 0, dst_col:dst_col + ncols], src, r=wreg, w=wt.all)
        else:
            P.dma(P.pool, view[:, :, dst_col:dst_col + ncols], src, r=wreg, w=wt.all)
    else:
        for b in range(4):
            sb = b + 1 if b % 2 == 0 else b - 1
            src = w_view(Wt, row0, nrows, col0 + sb * 48, 48)
            P.dma(P.pool, view[:, :, b * 48:(b + 1) * 48], src, r=wreg, w=wt.all)
    if cache is not None and key is not None and len(C.wslots) < NSLOT and dst_col == 0:
        slot = len(C.wslots)
        C.wslots[key] = slot
        cv = cache.t[slot, :pp, 0:kc * WCOLS].rearrange("p (kc c) -> p kc c", c=WCOLS)
        P.dma(P.pool, cv[:, :, 0:ncols], view[:, :, 0:ncols], r=wt.all, w=cache.r(slot))
    return wt, view


def stream_fm(P, C, Wt, wreg, K, col0, ncols, act, act_regs, nt, evac, group=WCOLS, chunk=128, perm48=False):
    KC = max(1, K // 128)
    ci = 0
    for g0 in range(0, ncols, group):
        gc = min(group, ncols - g0)
        wt, view = load_w(P, C, Wt, wreg, 0, K, col0 + g0, gc, perm48=perm48)
        for j0 in range(0, gc, chunk):
            m = min(chunk, gc - j0)
            ps = P.next_ps()
            for kc in range(KC):
                kp = min(128, K - kc * 128)
                P.op(P.pe, lambda e: e.matmul(ps[:m, :nt], lhsT=view[:kp, kc, j0:j0 + m], rhs=act[:kp, kc, :nt],
                                              start=(kc == 0), stop=(kc == KC - 1)),
                     r=wt.all + act_regs, w=ps.all, sig=(kc == KC - 1))
            evac(ci, g0 + j0, m, ps)
            ci += 1


def stream_tm(P, C, Wt, wreg, K, col0, ncols, act, act_regs, nt, evac):
    KC = K // 128
    for g0 in range(0, ncols, WCOLS):
        gc = min(WCOLS, ncols - g0)
        wt, view = load_w(P, C, Wt, wreg, 0, K, col0 + g0, gc)
        for ti in range(nt // 128):
            ps = P.next_ps()
            for kc in range(KC):
                P.op(P.pe, lambda e: e.matmul(ps[:, :gc], lhsT=act[:, kc, ti * 128:(ti + 1) * 128], rhs=view[:, kc, :gc],
                                              start=(kc == 0), stop=(kc == KC - 1)),
                     r=wt.all + act_regs, w=ps.all, sig=(kc == KC - 1))
            evac(ti, g0, gc, ps)


def rstd_from_ss(P, n, rstd, ps, nt, pp=128):
    P.op(P.dve, lambda e: e.tensor_scalar(out=rstd[:pp, :nt], in0=ps[:pp, :nt], scalar1=1.0 / n, scalar2=EPS,
                                          op0=ALU.mult, op1=ALU.add), r=ps.all, w=rstd.all)
    P.op(P.act, lambda e: e.activation(out=rstd[:pp, :nt], in_=rstd[:pp, :nt], func=AF.Sqrt), r=rstd.all, w=rstd.all)
    P.op(P.dve, lambda e: e.reciprocal(out=rstd[:pp, :nt], in_=rstd[:pp, :nt]), r=rstd.all, w=rstd.all)


def stage_init(P, C, cfg):
    with contextlib.ExitStack() as st:
        xin = [P.sbuf(f"xin{i}", [128, cfg.D], F32, stack=st) for i in range(2)]
        xo = [P.sbuf(f"xo{i}", [128, cfg.KCD, 128], F32, stack=st) for i in range(2)]
        for ti in range(cfg.T // 128):
            t0 = ti * 128
            a = xin[ti % 2]
            o = xo[ti % 2]
            if t0 < cfg.M:
                P.dma(P.sp, a[:], C.ctx_in[t0:t0 + 128, :], r=C.ctx_in.all, w=a.all)
            else:
                P.dma(P.sp, a[:], C.x_in[t0 - cfg.M:t0 - cfg.M + 128, :], r=C.x_in.all, w=a.all)
            for k4 in range(0, cfg.KCD, 4):
                ps = P.next_ps()
                n4 = min(4, cfg.KCD - k4)
                for q in range(n4):
                    kc = k4 + q
                    P.op(P.pe, lambda e: e.transpose(ps[:, q * 128:(q + 1) * 128], a[:, kc * 128:(kc + 1) * 128], C.ident[:]),
                         r=a.all + C.ident.all, w=ps.all, sig=(q == n4 - 1))
                E = P.ev_eng()
                src = ps[:, 0:n4 * 128].rearrange("p (k c) -> p k c", c=128)
                if E is P.act:
                    P.op(E, lambda e: e.copy(out=o[:, k4:k4 + n4, :], in_=src), r=ps.all, w=o.all)
                else:
                    P.op(E, lambda e: e.tensor_copy(out=o[:, k4:k4 + n4, :], in_=src), r=ps.all, w=o.all)
            P.dma(P.sp, C.HT.t.ap()[:, :, t0:t0 + 128].rearrange("k p t -> p k t"), o[:], r=o.all, w=C.HT.all)
    P.barrier()


def stage_final(P, C, cfg):
    with contextlib.ExitStack() as st:
        KC = cfg.KCD
        hb = P.sbuf("fin_h", [128, KC, 128], F32, stack=st)
        sq = [P.sbuf(f"fin_sq{i}", [128, 128], F32, stack=st) for i in range(2)]
        rstd = P.sbuf("fin_rstd", [128, 128], F32, stack=st)
        yn = [P.sbuf(f"fin_y{i}", [128, 128], F32, stack=st) for i in range(2)]
        ot = [P.sbuf(f"fin_o{i}", [128, cfg.D], F32, stack=st) for i in range(2)]
        for ti in range(cfg.L // 128):
            t0 = cfg.M + ti * 128
            P.dma(P.sp, hb[:], C.HT.t.ap()[:, :, t0:t0 + 128].rearrange("k p t -> p k t"), r=C.HT.all, w=hb.all)
            ps = P.next_ps()
            for kc in range(KC):
                s = sq[kc % 2]
                P.op(P.act, lambda e: e.activation(out=s[:], in_=hb[:, kc, :], func=AF.Square), r=hb.all, w=s.all)
                P.op(P.pe, lambda e: e.matmul(ps[:, :128], lhsT=C.ones[:], rhs=s[:], start=(kc == 0), stop=(kc == KC - 1)),
                     r=s.all + C.ones.all, w=ps.all, sig=True)
            rstd_from_ss(P, cfg.D, rstd, ps, 128)
            o = ot[ti % 2]
            for k4 in range(0, KC, 4):
                n4 = min(4, KC - k4)
                ps2 = P.next_ps()
                for q in range(n4):
                    kc = k4 + q
                    y = yn[kc % 2]
                    P.op(P.dve, lambda e: e.scalar_tensor_tensor(out=y[:], in0=hb[:, kc, :], scalar=C.fing[:, kc:kc + 1],
                                                                 in1=rstd[:], op0=ALU.mult, op1=ALU.mult),
                         r=hb.all + rstd.all + C.fing.all, w=y.all)
                    P.op(P.pe, lambda e: e.transpose(ps2[:, q * 128:(q + 1) * 128], y[:], C.ident[:]),
                         r=y.all + C.ident.all, w=ps2.all, sig=True)
                P.op(P.act, lambda e: e.copy(out=o[:, k4 * 128:(k4 + n4) * 128], in_=ps2[:, 0:n4 * 128]), r=ps2.all, w=o.all)
            P.dma(P.sp, C.out[ti * 128:(ti + 1) * 128, :], o[:], r=o.all, w=C.out.all)
    P.barrier()


def stage_mod(P, C, cfg, l):
    D, KC, R = cfg.D, cfg.KCD, cfg.RANK
    RC = R // 128
    with contextlib.ExitStack() as st:
        rT = P.sbuf("mod_rT", [128, RC, 2], BF16, stack=st)
        bias = P.sbuf("mod_bias", [128, 9, KC], F32, stack=st)
        ng = P.sbuf("mod_ng", [128, 3, KC], F32, stack=st)
        P.dma(P.sp, bias[:], C.mod_bias[l], r=C.mod_bias.all, w=bias.all)
        P.dma(P.sp, ng[:], C.norm_g[l], r=C.norm_g.all, w=ng.all)
        wA = []
        for rc in range(RC):
            wt, view = load_w(P, C, C.mod_a.t[l], C.mod_a.all, 0, D, rc * 128, 128, key=None)
            wA.append((wt, view))
        for rc in range(RC):
            wt, view = wA[rc]
            ps = P.next_ps()
            for kc in range(KC):
                P.op(P.pe, lambda e: e.matmul(ps[:, 0:2], lhsT=view[:, kc, 0:128], rhs=C.scb[:, kc, :],
                                              start=(kc == 0), stop=(kc == KC - 1)),
                     r=wt.all + C.scb.all, w=ps.all, sig=(kc == KC - 1))
            P.op(P.dve, lambda e: e.tensor_copy(out=rT[:, rc, :], in_=ps[:, 0:2]), r=ps.all, w=rT.all)
        ncols = 9 * D
        for g0 in range(0, ncols, WCOLS):
            wt, view = load_w(P, C, C.mod_b.t[l], C.mod_b.all, 0, R, g0, WCOLS, key=None)
            ps = P.next_ps()
            nch = WCOLS // 128
            for j in range(nch):
                for rc in range(RC):
                    P.op(P.pe, lambda e: e.matmul(ps[:, 2 * j:2 * j + 2], lhsT=view[:, rc, j * 128:(j + 1) * 128], rhs=rT[:, rc, :],
                                                  start=(rc == 0 and j == 0), stop=(rc == RC - 1),
                                                  skip_group_check=True),
                         r=wt.all + rT.all, w=ps.all, sig=(rc == RC - 1 and j == nch - 1))
            idx0 = g0 // 128
            dst = C.modt.t[:, :, :, :].rearrange("p j k c -> p (j k) c")[:, idx0:idx0 + nch, :]
            srcp = ps[:, 0:2 * nch].rearrange("p (a c) -> p a c", c=2)
            P.op(P.dve, lambda e: e.tensor_copy(out=dst, in_=srcp), r=ps.all, w=C.modt.all)
        for kind in range(2):
            P.op(P.dve, lambda e: e.tensor_tensor(out=C.modt[:, :, :, kind], in0=C.modt[:, :, :, kind], in1=bias[:],
                                                  op=ALU.add), r=C.modt.all + bias.all, w=C.modt.all)
        for i in range(3):
            for kind in range(2):
                P.op(P.dve, lambda e: e.scalar_tensor_tensor(out=C.Gs[:, i, :, kind], in0=C.modt[:, 3 * i + 1, :, kind],
                                                             scalar=1.0, in1=ng[:, i, :], op0=ALU.add, op1=ALU.mult),
                     r=C.modt.all + ng.all, w=C.Gs.all)
                P.op(P.dve, lambda e: e.tensor_copy(out=C.SHs[:, i, :, kind], in_=C.modt[:, 3 * i, :, kind]),
                     r=C.modt.all, w=C.SHs.all)
                P.op(P.dve, lambda e: e.tensor_scalar(out=C.GTs[:, i, :, kind], in0=C.modt[:, 3 * i + 2, :, kind],
                                                      scalar1=(1.0 if i == 1 else 0.5), scalar2=None, op0=ALU.mult),
                     r=C.modt.all, w=C.GTs.all)
    P.barrier()


def norm_block(P, C, cfg, st_tiles, i, t0, nt, kind, uT, blk):
    KC = cfg.KCD
    hbuf, sq, rstd, tmp = st_tiles
    hreg = C.HT.r(blk)
    ps = P.next_ps()
    for kc in range(KC):
        hc = hbuf[kc % len(hbuf)]
        P.dma(P.sp, hc[:, :nt], C.HT[kc, :, t0:t0 + nt], r=hreg, w=hc.all)
        s = sq[kc % 2]
        P.op(P.act, lambda e: e.activation(out=s[:, :nt], in_=hc[:, :nt], func=AF.Square), r=hc.all, w=s.all)
        P.op(P.pe, lambda e: e.matmul(ps[:, :nt], lhsT=C.ones[:], rhs=s[:, :nt], start=(kc == 0), stop=(kc == KC - 1)),
             r=s.all + C.ones.all, w=ps.all, sig=True)
    rstd_from_ss(P, cfg.D, rstd, ps, nt)
    for kc in range(KC):
        hc = hbuf[kc % len(hbuf)]
        P.dma(P.sp, hc[:, :nt], C.HT[kc, :, t0:t0 + nt], r=hreg, w=hc.all)
        tt = tmp[kc % 2]
        P.op(P.dve, lambda e: e.scalar_tensor_tensor(out=tt[:, :nt], in0=hc[:, :nt], scalar=C.Gs[:, i, kc, kind:kind + 1],
                                                     in1=rstd[:, :nt], op0=ALU.mult, op1=ALU.mult),
             r=hc.all + rstd.all + C.Gs.all, w=tt.all)
        P.op(P.act, lambda e: e.activation(out=uT[:, kc, :nt], in_=tt[:, :nt], func=AF.Identity,
                                           bias=C.SHs[:, i, kc, kind:kind + 1], scale=1.0),
             r=tt.all + C.SHs.all, w=uT.all)


def alloc_norm_tiles(P, st, pfx):
    hbuf = [P.sbuf(f"{pfx}_h{i}", [128, 512], F32, stack=st) for i in range(3)]
    sq = [P.sbuf(f"{pfx}_sq{i}", [128, 512], F32, stack=st) for i in range(2)]
    rstd = P.sbuf(f"{pfx}_rstd", [128, 512], F32, stack=st)
    tmp = [P.sbuf(f"{pfx}_tmp{i}", [128, 512], F32, stack=st) for i in range(2)]
    return hbuf, sq, rstd, tmp


def stage_ffn(P, C, cfg, l, which):
    i = 0 if which == 0 else 2
    D, FF, KC = cfg.D, cfg.FF, cfg.KCD
    FC = FF // 128
    Wup = C.ffn_up.t[l, which]
    Wdn = C.ffn_down.t[l, which]
    C.wslots = {}
    with contextlib.ExitStack() as st:
        nt_tiles = alloc_norm_tiles(P, st, "ffn")
        uT = P.sbuf("ffn_uT", [128, KC, 512], BF16, stack=st)
        hm = P.sbuf("ffn_hm", [128, FC, 512], BF16, stack=st)
        sa = [P.sbuf(f"ffn_sa{k}", [128, 512], F32, stack=st) for k in range(2)]
        hres = [P.sbuf(f"ffn_hr{k}", [128, 512], F32, stack=st) for k in range(2)]
        hnew = [P.sbuf(f"ffn_hn{k}", [128, 512], F32, stack=st) for k in range(2)]
        for blk, (t0, nt, kind) in enumerate(cfg.blocks):
            if kind == 1 and l == cfg.DEPTH - 1 and which == 1:
                continue
            norm_block(P, C, cfg, nt_tiles, i, t0, nt, kind, uT, blk)
            n = 0
            for g0 in range(0, FF, WCOLS):
                wa, va = load_w(P, C, Wup, C.ffn_up.all, 0, D, g0, WCOLS)
                wb, vb = load_w(P, C, Wup, C.ffn_up.all, 0, D, FF + g0, WCOLS)
                for j in range(WCOLS // 128):
                    psa = P.next_ps()
                    psb = P.next_ps()
                    for kc in range(KC):
                        P.op(P.pe, lambda e: e.matmul(psa[:, :nt], lhsT=va[:, kc, j * 128:(j + 1) * 128], rhs=uT[:, kc, :nt],
                                                      start=(kc == 0), stop=(kc == KC - 1)),
                             r=wa.all + uT.all, w=psa.all, sig=(kc == KC - 1))
                    for kc in range(KC):
                        P.op(P.pe, lambda e: e.matmul(psb[:, :nt], lhsT=vb[:, kc, j * 128:(j + 1) * 128], rhs=uT[:, kc, :nt],
                                                      start=(kc == 0), stop=(kc == KC - 1)),
                             r=wb.all + uT.all, w=psb.all, sig=(kc == KC - 1))
                    s = sa[n % 2]
                    fc = (g0 // 128) + j
                    P.op(P.act, lambda e: e.activation(out=s[:, :nt], in_=psa[:, :nt], func=AF.Silu), r=psa.all, w=s.all)
                    P.op(P.dve, lambda e: e.tensor_tensor(out=hm[:, fc, :nt], in0=s[:, :nt], in1=psb[:, :nt], op=ALU.mult),
                         r=s.all + psb.all, w=hm.all)
                    n += 1
            hreg = C.HT.r(blk)

            def evac(ci, coff, m, ps):
                kc = coff // 128
                hr = hres[ci % 2]
                hn = hnew[ci % 2]
                P.dma(P.sp, hr[:, :nt], C.HT[kc, :, t0:t0 + nt], r=hreg, w=hr.all)
                P.op(P.dve, lambda e: e.scalar_tensor_tensor(out=hn[:, :nt], in0=ps[:, :nt], scalar=C.GTs[:, i, kc, kind:kind + 1],
                                                             in1=hr[:, :nt], op0=ALU.mult, op1=ALU.add),
                     r=ps.all + hr.all + C.GTs.all, w=hn.all)
                P.dma(P.sp, C.HT[kc, :, t0:t0 + nt], hn[:, :nt], r=hn.all, w=hreg)

            stream_fm(P, C, Wdn, C.ffn_down.all, FF, 0, D, hm, hm.all, nt, evac)
    P.barrier()


def declare_io(P, cfg, C):
    D, KC = cfg.D, cfg.KCD
    ext = lambda n, s: P.dram(n, s, F32, kind="ExternalInput")
    C.x_in = ext("x_in", [cfg.L, D])
    C.ctx_in = ext("ctx_in", [cfg.M, D])
    C.cvec = ext("cvec", [128, KC, 2])
    C.mod_a = ext("mod_a", [cfg.DEPTH, D, cfg.RANK])
    C.mod_b = ext("mod_b", [cfg.DEPTH, cfg.RANK, 9 * D])
    C.mod_bias = ext("mod_bias", [cfg.DEPTH, 128, 9, KC])
    C.norm_g = ext("norm_g", [cfg.DEPTH, 128, 3, KC])
    C.ffn_up = ext("ffn_up", [cfg.DEPTH, 2, D, 2 * cfg.FF])
    C.ffn_down = ext("ffn_down", [cfg.DEPTH, 2, cfg.FF, D])
    C.w_in = ext("w_in", [cfg.DEPTH, D, cfg.NIN])
    C.conv_w = ext("conv_w", [cfg.DEPTH, 128, cfg.CW // 128, 3])
    C.decay_w = ext("decay_w", [cfg.DEPTH, 2, 17, cfg.GK])
    C.gla_ng = ext("gla_ng", [cfg.DEPTH, 128, cfg.GV // 128])
    C.rpbT = ext("rpbT", [cfg.DEPTH, 31, cfg.NH * 15])
    C.wb_conv = ext("wb_conv", [cfg.DEPTH, cfg.CW, D])
    C.wb_gla = ext("wb_gla", [cfg.DEPTH, cfg.GV, D])
    C.wb_na = ext("wb_na", [cfg.DEPTH, cfg.NW, D])
    C.w_out = ext("w_out", [cfg.DEPTH, D, D])
    C.fing_in = ext("final_g", [128, KC])
    C.consts = ext("consts", [128, NCONST])
    C.rope = ext("rope", [4, 192, cfg.L])
    C.nasel = ext("nasel", [31, 4096])
    C.out = P.dram("out", [cfg.L, D], F32, kind="ExternalOutput")


C_IDENT, C_ONES, C_TRIF, C_TRIB, C_STRF, C_STRB, C_IND, C_CMASK = 0, 128, 256, 384, 512, 640, 768, 770
NCONST = 770 + 64


def host_consts():
    c = np.zeros((128, NCONST), np.float32)
    c[:, C_IDENT:C_IDENT + 128] = np.eye(128)
    c[:, C_ONES:C_ONES + 128] = 1.0
    s = np.arange(128)[:, None]
    t = np.arange(128)[None, :]
    same = (s // 64) == (t // 64)
    c[:, C_TRIF:C_TRIF + 128] = (same & (s <= t))
    c[:, C_TRIB:C_TRIB + 128] = (same & (s >= t))
    c[:, C_STRF:C_STRF + 128] = (same & (s > t))
    c[:, C_STRB:C_STRB + 128] = (same & (s < t))
    c[:, C_IND] = (np.arange(128) < 64)
    c[:, C_IND + 1] = (np.arange(128) >= 64)
    kc_ = np.arange(64)[:, None]
    qc_ = np.arange(64)[None, :]
    win = np.clip(qc_ - 8, 0, 48)
    valid = (kc_ >= win) & (kc_ < win + 16)
    c[:64, C_CMASK:C_CMASK + 64] = np.where(valid, 0.0, -30000.0)
    c[64:, C_CMASK:C_CMASK + 64] = np.where(valid, 0.0, -30000.0)
    return c


def build(cfg, n_layers=None, stages=("ffn0", "mix", "ffn1")):
    nc = bass.Bass("TRN2", target_bir_lowering=False)
    P = Prog(nc)
    C = Ctx()
    declare_io(P, cfg, C)
    D, KC = cfg.D, cfg.KCD
    NB = len(cfg.blocks)
    C.HT = P.dram("HT", [KC, 128, cfg.T], F32, nreg=NB)
    P.ps_tiles = [P.psum(f"ps{i}", [128, 512], F32) for i in range(8)]
    C.wbufs = [P.sbuf(f"wbuf{i}", [128, WBUF_ELEMS], BF16) for i in range(4)]
    C.wnext = 0
    C.wslots = {}
    if USE_WCACHE:
        C.wcache = P.dram("wcache", [NSLOT, 128, WBUF_ELEMS], BF16, nreg=NSLOT)
    C.cst = P.sbuf("cst", [128, NCONST], F32)
    P.dma(P.sp, C.cst[:], C.consts[:], r=C.consts.all, w=C.cst.all)

    class V:
        def __init__(self, a, b):
            self.a, self.b = a, b
            self.all = C.cst.all

        def __getitem__(self, k):
            return C.cst.t[:, self.a:self.b][k]
    C.ident = V(C_IDENT, C_IDENT + 128)
    C.ones = V(C_ONES, C_ONES + 128)
    C.V = V
    C.fing = P.sbuf("fing", [128, KC], F32)
    P.dma(P.sp, C.fing[:], C.fing_in[:], r=C.fing_in.all, w=C.fing.all)
    C.modt = P.sbuf("modt", [128, 9, KC, 2], F32)
    C.Gs = P.sbuf("Gs", [128, 3, KC, 2], F32)
    C.SHs = P.sbuf("SHs", [128, 3, KC, 2], F32)
    C.GTs = P.sbuf("GTs", [128, 3, KC, 2], F32)
    cv = P.sbuf("cv", [128, KC, 2], F32)
    C.scb = P.sbuf("scb", [128, KC, 2], BF16)
    P.dma(P.sp, cv[:], C.cvec[:], r=C.cvec.all, w=cv.all)
    P.op(P.act, lambda e: e.activation(out=C.scb[:], in_=cv[:], func=AF.Silu), r=cv.all, w=C.scb.all)

    alloc_scratch(P, C, cfg)
    stage_init(P, C, cfg)
    nl = cfg.DEPTH if n_layers is None else n_layers
    for l in range(nl):
        stage_mod(P, C, cfg, l)
        if "ffn0" in stages:
            stage_ffn(P, C, cfg, l, 0)
        if "mix" in stages:
            stage_mixer(P, C, cfg, l)
        if "ffn1" in stages:
            stage_ffn(P, C, cfg, l, 1)
    stage_final(P, C, cfg)
    P.barrier()
    P.es.close()
    return nc, P


BRANCHES = ("A", "B", "C")


def stage_mixer(P, C, cfg, l):
    stage_proj(P, C, cfg, l)
    if "A" in BRANCHES:
        stage_conv(P, C, cfg, l)
    if "B" in BRANCHES:
        stage_gla(P, C, cfg, l)
    if "C" in BRANCHES:
        stage_na(P, C, cfg, l)
    stage_combine(P, C, cfg, l)


def alloc_scratch(P, C, cfg):
    NB = len(cfg.blocks)
    T, KC = cfg.T, cfg.KCD
    C.PC = P.dram("PC", [3 * cfg.CW // 128, 128, T], F32, nreg=NB)
    C.PQ = P.dram("PQ", [2, cfg.GH, 192, T], F32, nreg=NB)
    C.PV = P.dram("PV", [T, cfg.GV], BF16, nreg=NB)
    C.PG = P.dram("PG", [cfg.GV // 128, 128, T], F32, nreg=NB)
    C.PLR = P.dram("PLR", [2, 16, T], F32, nreg=NB)
    C.PNQ = P.dram("PNQ", [cfg.NH, 128, T], BF16, nreg=NB)
    C.PNK = P.dram("PNK", [cfg.NH, 128, T], BF16, nreg=NB)
    C.PNV = P.dram("PNV", [T, cfg.NW], BF16, nreg=NB)
    C.PGA = P.dram("PGA", [3, KC, 128, T], F32, nreg=NB)
    C.YA = P.dram("YA", [cfg.CW // 128, 128, T], BF16, nreg=NB)
    C.YB = P.dram("YB", [cfg.GV // 128, 128, T], BF16, nreg=NB)
    C.YC = P.dram("YC", [cfg.NW // 128, 128, T], BF16, nreg=NB)
    C.OF = P.dram("OF", [cfg.GV // 128, 128, T], F32, nreg=1)
    C.NABT = P.dram("NABT", [cfg.NH * 15, 4096], F32, nreg=1)


def stage_proj(P, C, cfg, l):
    D, KC = cfg.D, cfg.KCD
    W = C.w_in.t[l]
    wreg = C.w_in.all
    C.wslots = {}
    with contextlib.ExitStack() as st:
        nt_tiles = alloc_norm_tiles(P, st, "pj")
        uT = P.sbuf("pj_uT", [128, KC, 512], BF16, stack=st)
        s32 = [P.sbuf(f"pj_s32_{k}", [128, 512], F32, stack=st) for k in range(4)]
        s16 = [P.sbuf(f"pj_s16_{k}", [128, 512], BF16, stack=st) for k in range(4)]
        rp = [P.sbuf(f"pj_rope{k}", [128, 512], F32, stack=st) for k in range(4)]
        t1 = [P.sbuf(f"pj_t1_{k}", [128, 512], F32, stack=st) for k in range(2)]
        t2 = [P.sbuf(f"pj_t2_{k}", [128, 512], F32, stack=st) for k in range(2)]
        cnt = [0]

        def stg(pool):
            cnt[0] += 1
            return pool[cnt[0] % len(pool)]

        for blk, (t0, nt, kind) in enumerate(cfg.blocks):
            norm_block(P, C, cfg, nt_tiles, 1, t0, nt, kind, uT, blk)

            def ev_fm(dst_fn, func=None, scale=None, dt32=True):
                def evac(ci, coff, m, ps):
                    o = stg(s32 if dt32 else s16)
                    if func is not None:
                        P.op(P.act, lambda e: e.activation(out=o[:m, :nt], in_=ps[:m, :nt], func=func), r=ps.all, w=o.all)
                    elif scale is not None:
                        P.op(P.act, lambda e: e.mul(out=o[:m, :nt], in_=ps[:m, :nt], mul=scale), r=ps.all, w=o.all)
                    else:
                        E = P.ev_eng()
                        if E is P.act:
                            P.op(E, lambda e: e.copy(out=o[:m, :nt], in_=ps[:m, :nt]), r=ps.all, w=o.all)
                        else:
                            P.op(E, lambda e: e.tensor_copy(out=o[:m, :nt], in_=ps[:m, :nt]), r=ps.all, w=o.all)
                    dst, dreg = dst_fn(ci, m)
                    P.dma(P.sp, dst, o[:m, :nt], r=o.all, w=dreg)
                return evac

            def ev_tm(Dst):
                def evac(ti, g0, gc, ps):
                    o = stg(s16)
                    E = P.ev_eng()
                    if E is P.act:
                        P.op(E, lambda e: e.copy(out=o[:, :gc], in_=ps[:, :gc]), r=ps.all, w=o.all)
                    else:
                        P.op(E, lambda e: e.tensor_copy(out=o[:, :gc], in_=ps[:, :gc]), r=ps.all, w=o.all)
                    P.dma(P.sp, Dst[t0 + ti * 128:t0 + (ti + 1) * 128, g0:g0 + gc], o[:, :gc], r=o.all, w=Dst.r(blk))
                return evac

            stream_fm(P, C, W, wreg, D, cfg.off["h"], 3 * cfg.CW, uT, uT.all, nt,
                      ev_fm(lambda ci, m: (C.PC[ci, :m, t0:t0 + nt], C.PC.r(blk))))
            for which, nm in enumerate(("q", "k")):
                if kind == 0:
                    lt0 = t0 - cfg.M
                    for k4, (ti, j0, m) in enumerate(((2 * which, 0, 128), (2 * which, 128, 64),
                                                      (2 * which + 1, 0, 128), (2 * which + 1, 128, 64))):
                        P.dma(P.sp, rp[k4][:m, :nt], C.rope[ti, j0:j0 + m, lt0:lt0 + nt], r=C.rope.all, w=rp[k4].all)
                for hd in range(cfg.GH):
                    col = cfg.off[nm] + hd * 192
                    wt, view = load_w(P, C, W, wreg, 0, D, col, 192)
                    if kind == 0:
                        wr, vr = load_w(P, C, W, wreg, 0, D, col, 192, perm48=True)
                    for cix, (j0, m) in enumerate(((0, 128), (128, 64))):
                        ps = P.next_ps()
                        for kc in range(KC):
                            P.op(P.pe, lambda e: e.matmul(ps[:m, :nt], lhsT=view[:, kc, j0:j0 + m], rhs=uT[:, kc, :nt],
                                                          start=(kc == 0), stop=(kc == KC - 1)),
                                 r=wt.all + uT.all, w=ps.all, sig=(kc == KC - 1))
                        o = stg(s32)
                        if kind == 0:
                            ps2 = P.next_ps()
                            for kc in range(KC):
                                P.op(P.pe, lambda e: e.matmul(ps2[:m, :nt], lhsT=vr[:, kc, j0:j0 + m], rhs=uT[:, kc, :nt],
                                                              start=(kc == 0), stop=(kc == KC - 1)),
                                     r=wr.all + uT.all, w=ps2.all, sig=(kc == KC - 1))
                            a = t1[cix]
                            b_ = t2[cix]
                            cosT, sinT = rp[cix], rp[2 + cix]
                            P.op(P.dve, lambda e: e.tensor_tensor(out=a[:m, :nt], in0=ps[:m, :nt], in1=cosT[:m, :nt], op=ALU.mult),
                                 r=ps.all + cosT.all, w=a.all)
                            P.op(P.dve, lambda e: e.tensor_tensor(out=b_[:m, :nt], in0=ps2[:m, :nt], in1=sinT[:m, :nt], op=ALU.mult),
                                 r=ps2.all + sinT.all, w=b_.all)
                            P.op(P.dve, lambda e: e.tensor_tensor(out=o[:m, :nt], in0=a[:m, :nt], in1=b_[:m, :nt], op=ALU.add),
                                 r=a.all + b_.all, w=o.all)
                        else:
                            sc_ = float(cfg.DK ** -0.5) if which == 0 else 1.0
                            P.op(P.act, lambda e: e.mul(out=o[:m, :nt], in_=ps[:m, :nt], mul=sc_), r=ps.all, w=o.all)
                        P.dma(P.sp, C.PQ[which, hd, j0:j0 + m, t0:t0 + nt], o[:m, :nt], r=o.all, w=C.PQ.r(blk))
            stream_tm(P, C, W, wreg, D, cfg.off["v"], cfg.GV, uT, uT.all, nt, ev_tm(C.PV))
            stream_fm(P, C, W, wreg, D, cfg.off["g"], cfg.GV, uT, uT.all, nt,
                      ev_fm(lambda ci, m: (C.PG[ci, :m, t0:t0 + nt], C.PG.r(blk)), func=AF.Silu))
            stream_fm(P, C, W, wreg, D, cfg.off["lr"], 32, uT, uT.all, nt,
                      ev_fm(lambda ci, m: (C.PLR[ci, :m, t0:t0 + nt], C.PLR.r(blk))), group=32, chunk=16)
            stream_fm(P, C, W, wreg, D, cfg.off["nq"], cfg.NW, uT, uT.all, nt,
                      ev_fm(lambda ci, m: (C.PNQ[ci, :m, t0:t0 + nt], C.PNQ.r(blk)), scale=float(cfg.HD ** -0.5), dt32=False))
            stream_fm(P, C, W, wreg, D, cfg.off["nk"], cfg.NW, uT, uT.all, nt,
                      ev_fm(lambda ci, m: (C.PNK[ci, :m, t0:t0 + nt], C.PNK.r(blk)), dt32=False))
            stream_tm(P, C, W, wreg, D, cfg.off["nv"], cfg.NW, uT, uT.all, nt, ev_tm(C.PNV))
            if not (kind == 1 and l == cfg.DEPTH - 1):
                stream_fm(P, C, W, wreg, D, cfg.off["ga"], 3 * D, uT, uT.all, nt,
                          ev_fm(lambda ci, m: (C.PGA[ci // KC, ci % KC, :m, t0:t0 + nt], C.PGA.r(blk)), func=AF.Sigmoid))
    P.barrier()


def stage_conv(P, C, cfg, l):
    NCH = cfg.CW // 128
    with contextlib.ExitStack() as st:
        cw = P.sbuf("cv_w", [128, NCH, 3], F32, stack=st)
        P.dma(P.sp, cw[:], C.conv_w[l], r=C.conv_w.all, w=cw.all)
        hh = [P.sbuf(f"cv_h{k}", [128, 514], F32, stack=st) for k in range(2)]
        cg = [P.sbuf(f"cv_cg{k}", [128, 514], F32, stack=st) for k in range(2)]
        bg = [P.sbuf(f"cv_bg{k}", [128, 512], F32, stack=st) for k in range(2)]
        zt = [P.sbuf(f"cv_z{k}", [128, 514], F32, stack=st) for k in range(2)]
        oo = [P.sbuf(f"cv_o{k}", [128, 512], F32, stack=st) for k in range(2)]
        yy = [P.sbuf(f"cv_y{k}", [128, 512], BF16, stack=st) for k in range(2)]
        n = 0
        for blk, (t0, nt, kind) in enumerate(cfg.blocks):
            if kind == 1 and l == cfg.DEPTH - 1:
                continue
            s0, s1 = (0, cfg.M) if kind == 1 else (cfg.M, cfg.T)
            lo, hi = max(t0 - 1, s0), min(t0 + nt + 1, s1)
            a, b = lo - (t0 - 1), hi - (t0 - 1)
            rregs = [C.PC.regs[bb] for bb in range(len(cfg.blocks))
                     if cfg.blocks[bb][0] < hi and cfg.blocks[bb][0] + cfg.blocks[bb][1] > lo]
            for c in range(NCH):
                h_, c_, b_, z_, o_, y_ = hh[n % 2], cg[n % 2], bg[n % 2], zt[n % 2], oo[n % 2], yy[n % 2]
                n += 1
                P.dma(P.sp, h_[:, a:b], C.PC[c, :, lo:hi], r=rregs, w=h_.all)
                P.dma(P.sp, c_[:, a:b], C.PC[2 * NCH + c, :, lo:hi], r=rregs, w=c_.all)
                P.dma(P.sp, b_[:, :nt], C.PC[NCH + c, :, t0:t0 + nt], r=rregs, w=b_.all)
                P.op(P.dve, lambda e: e.memset(z_[:, 0:nt + 2], 0.0), w=z_.all)
                P.op(P.dve, lambda e: e.tensor_tensor(out=z_[:, a:b], in0=h_[:, a:b], in1=c_[:, a:b], op=ALU.mult),
                     r=h_.all + c_.all, w=z_.all)
                P.op(P.dve, lambda e: e.tensor_scalar(out=o_[:, :nt], in0=z_[:, 0:nt], scalar1=cw[:, c, 0:1], scalar2=None,
                                                      op0=ALU.mult), r=z_.all + cw.all, w=o_.all)
                for k in (1, 2):
                    P.op(P.dve, lambda e: e.scalar_tensor_tensor(out=o_[:, :nt], in0=z_[:, k:nt + k], scalar=cw[:, c, k:k + 1],
                                                                 in1=o_[:, :nt], op0=ALU.mult, op1=ALU.add),
                         r=z_.all + cw.all + o_.all, w=o_.all)
                P.op(P.dve, lambda e: e.tensor_tensor(out=y_[:, :nt], in0=o_[:, :nt], in1=b_[:, :nt], op=ALU.mult),
                     r=o_.all + b_.all, w=y_.all)
                P.dma(P.sp, C.YA[c, :, t0:t0 + nt], y_[:, :nt], r=y_.all, w=C.YA.r(blk))
    P.barrier()


def stage_combine(P, C, cfg, l):
    D, KC = cfg.D, cfg.KCD
    C.wslots = {}
    brs = []
    if "A" in BRANCHES:
        brs.append((0, C.YA, C.wb_conv, cfg.CW // 128))
    if "B" in BRANCHES:
        brs.append((1, C.YB, C.wb_gla, cfg.GV // 128))
    if "C" in BRANCHES:
        brs.append((2, C.YC, C.wb_na, cfg.NW // 128))
    with contextlib.ExitStack() as st:
        sacc = P.sbuf("cb_sacc", [128, KC, 512], F32, stack=st)
        sbf = P.sbuf("cb_sbf", [128, KC, 512], BF16, stack=st)
        act = P.sbuf("cb_act", [128, 12, 512], BF16, stack=st)
        gt = [P.sbuf(f"cb_g{k}", [128, 512], F32, stack=st) for k in range(2)]
        tm = [P.sbuf(f"cb_t{k}", [128, 512], F32, stack=st) for k in range(2)]
        hres = [P.sbuf(f"cb_hr{k}", [128, 512], F32, stack=st) for k in range(2)]
        hnew = [P.sbuf(f"cb_hn{k}", [128, 512], F32, stack=st) for k in range(2)]
        for blk, (t0, nt, kind) in enumerate(cfg.blocks):
            if kind == 1 and l == cfg.DEPTH - 1:
                continue
            for bi, (X, Y, Wb, nK) in enumerate(brs):
                P.dma(P.sp, act[:, 0:nK, :nt], Y.t.ap()[:, :, t0:t0 + nt].rearrange("k p t -> p k t"), r=Y.r(blk), w=act.all)

                def evac(ci, coff, m, ps, X=X, bi=bi):
                    g = gt[ci % 2]
                    P.dma(P.sp, g[:, :nt], C.PGA[X, ci, :, t0:t0 + nt], r=C.PGA.r(blk), w=g.all)
                    if bi == 0:
                        P.op(P.dve, lambda e: e.tensor_tensor(out=sacc[:, ci, :nt], in0=ps[:, :nt], in1=g[:, :nt], op=ALU.mult),
                             r=ps.all + g.all, w=sacc.all)
                    else:
                        t_ = tm[ci % 2]
                        P.op(P.dve, lambda e: e.tensor_tensor(out=t_[:, :nt], in0=ps[:, :nt], in1=g[:, :nt], op=ALU.mult),
                             r=ps.all + g.all, w=t_.all)
                        P.op(P.dve, lambda e: e.tensor_tensor(out=sacc[:, ci, :nt], in0=sacc[:, ci, :nt], in1=t_[:, :nt], op=ALU.add),
                             r=sacc.all + t_.all, w=sacc.all)
                    if bi == len(brs) - 1:
                        P.op(P.act, lambda e: e.copy(out=sbf[:, ci, :nt], in_=sacc[:, ci, :nt]), r=sacc.all, w=sbf.all)

                stream_fm(P, C, Wb.t[l], Wb.all, nK * 128, 0, D, act, act.all, nt, evac)

            hreg = C.HT.r(blk)

            def evac2(ci, coff, m, ps):
                hr = hres[ci % 2]
                hn = hnew[ci % 2]
                P.dma(P.sp, hr[:, :nt], C.HT[ci, :, t0:t0 + nt], r=hreg, w=hr.all)
                P.op(P.dve, lambda e: e.scalar_tensor_tensor(out=hn[:, :nt], in0=ps[:, :nt], scalar=C.GTs[:, 1, ci, kind:kind + 1],
                                                             in1=hr[:, :nt], op0=ALU.mult, op1=ALU.add),
                     r=ps.all + hr.all + C.GTs.all, w=hn.all)
                P.dma(P.sp, C.HT[ci, :, t0:t0 + nt], hn[:, :nt], r=hn.all, w=hreg)

            stream_fm(P, C, C.w_out.t[l], C.w_out.all, D, 0, D, sbf, sbf.all, nt, evac2)
    P.barrier()


def stage_gla(P, C, cfg, l):
    M, T = cfg.M, cfg.T
    NT = T // 128
    GH = cfg.GH
    nctx = M // 128
    with contextlib.ExitStack() as st:
        sb = lambda n, shp, dt=F32: P.sbuf("gl_" + n, shp, dt, stack=st)
        waug = sb("waug", [17, 2, cfg.GK])
        P.dma(P.sp, waug[:], C.decay_w.t[l].rearrange("d k n -> k d n"), r=C.decay_w.all, w=waug.all)
        ng = sb("ng", [128, cfg.GV // 128])
        P.dma(P.sp, ng[:], C.gla_ng[l], r=C.gla_ng.all, w=ng.all)
        lra = [sb(f"lra{k}", [32, 128]) for k in range(2)]
        for t_ in lra:
            P.op(P.dve, lambda e: e.memset(t_[:], 1.0), w=t_.all)
        e1 = [sb(f"e1_{k}", [128, cfg.GK]) for k in range(2)]
        nla = [sb(f"nla{k}", [128, cfg.GK]) for k in range(2)]
        qA = [sb(f"qA{k}", [128, GH, 128]) for k in range(2)]
        qB = [sb(f"qB{k}", [64, GH, 128]) for k in range(2)]
        kA = [sb(f"kA{k}", [128, GH, 128]) for k in range(2)]
        kB = [sb(f"kB{k}", [64, GH, 128]) for k in range(2)]
        vt = [sb(f"v{k}", [128, cfg.GV], BF16) for k in range(2)]
        eq = [sb(f"eq{k}", [128, 2, 128]) for k in range(2)]
        ek = [sb(f"ek{k}", [128, 2, 128]) for k in range(2)]
        qd = [sb(f"qd{k}", [128, 2, 128], BF16) for k in range(2)]
        ki = [sb(f"ki{k}", [128, 2, 128], BF16) for k in range(2)]
        dk_ = [sb(f"dk{k}", [128, 192]) for k in range(2)]
        kd = [sb(f"kd{k}", [128, 192], BF16) for k in range(2)]
        dec = [sb(f"dec{k}", [128, 4]) for k in range(2)]
        am = [sb(f"am{k}", [128, 128], BF16) for k in range(2)]
        SA = [sb(f"SA{h}", [128, 384]) for h in range(GH)]
        SB = [sb(f"SB{h}", [64, 384]) for h in range(GH)]
        SAb = [sb(f"SAb{h}", [128, 384], BF16) for h in range(GH)]
        SBb = [sb(f"SBb{h}", [64, 384], BF16) for h in range(GH)]
        osum = [sb(f"osum{k}", [128, 3, 128]) for k in range(2)]
        ofl = [sb(f"ofl{k}", [128, 3, 128]) for k in range(2)]
        sq = [sb(f"sq{k}", [128, 3, 128]) for k in range(2)]
        sg = [sb(f"sg{k}", [128, 3, 128]) for k in range(2)]
        rstd = [sb(f"rstd{k}", [128, 128]) for k in range(2)]
        tt = [sb(f"tt{k}", [128, 128]) for k in range(2)]
        yb = [sb(f"yb{k}", [128, 3, 128], BF16) for k in range(2)]
        V = C.V
        cnt = 0
        for dr in range(2):
            TRI = V(C_TRIF, C_TRIF + 128) if dr == 0 else V(C_TRIB, C_TRIB + 128)
            STR = V(C_STRF, C_STRF + 128) if dr == 0 else V(C_STRB, C_STRB + 128)
            IND = V(C_IND, C_IND + 2)
            for h in range(GH):
                P.op(P.dve, lambda e: e.memset(SA[h][:], 0.0), w=SA[h].all)
                P.op(P.dve, lambda e: e.memset(SB[h][:], 0.0), w=SB[h].all)
                P.op(P.dve, lambda e: e.memset(SAb[h][:], 0.0), w=SAb[h].all)
                P.op(P.dve, lambda e: e.memset(SBb[h][:], 0.0), w=SBb[h].all)
            if dr == 0:
                order = list(range(NT))
            else:
                order = list(range(nctx - 1, -1, -1)) + list(range(NT - 1, nctx - 1, -1))
            chunks = (0, 1) if dr == 0 else (1, 0)
            for ti, n in enumerate(order):
                t0 = n * 128
                b2 = ti % 2
                la_, e1_, nl_ = lra[b2], e1[b2], nla[b2]
                P.dma(P.sp, la_[0:16, :], C.PLR[dr, :, t0:t0 + 128], r=C.PLR.all, w=la_.all)
                for hf in range(2):
                    ps = P.next_ps()
                    P.op(P.pe, lambda e: e.matmul(ps[:, :384], lhsT=la_[0:17, :], rhs=waug[0:17, dr, hf * 384:(hf + 1) * 384],
                                                  start=True, stop=True), r=la_.all + waug.all, w=ps.all)
                    P.op(P.act, lambda e: e.activation(out=e1_[:, hf * 384:(hf + 1) * 384], in_=ps[:, :384], func=AF.Exp, scale=-1.0),
                         r=ps.all, w=e1_.all)
                P.op(P.act, lambda e: e.activation(out=nl_[:], in_=e1_[:], func=AF.Ln, bias=1.0, scale=1.0), r=e1_.all, w=nl_.all)
                qa, qb_, ka, kb_, v_ = qA[b2], qB[b2], kA[b2], kB[b2], vt[b2]
                P.dma(P.sp, qa[:], C.PQ.t.ap()[0, :, 0:128, t0:t0 + 128].rearrange("h p t -> p h t"), r=C.PQ.all, w=qa.all)
                P.dma(P.sp, qb_[:], C.PQ.t.ap()[0, :, 128:192, t0:t0 + 128].rearrange("h p t -> p h t"), r=C.PQ.all, w=qb_.all)
                P.dma(P.sp, ka[:], C.PQ.t.ap()[1, :, 0:128, t0:t0 + 128].rearrange("h p t -> p h t"), r=C.PQ.all, w=ka.all)
                P.dma(P.sp, kb_[:], C.PQ.t.ap()[1, :, 128:192, t0:t0 + 128].rearrange("h p t -> p h t"), r=C.PQ.all, w=kb_.all)
                P.dma(P.sp, v_[:], C.PV[t0:t0 + 128, :], r=C.PV.all, w=v_.all)
                for h in range(GH):
                    cnt += 1
                    c2 = cnt % 2
                    d0 = h * 192
                    eq_, ek_, qd_, ki_, dkk, kd_, dec_, am_ = eq[c2], ek[c2], qd[c2], ki[c2], dk_[c2], kd[c2], dec[c2], am[c2]
                    pc = P.next_ps()
                    P.op(P.pe, lambda e: e.matmul(pc[:, 0:128], lhsT=nl_[:, d0:d0 + 128], rhs=TRI[:], start=True, stop=True),
                         r=nl_.all + TRI.all, w=pc.all, sig=False)
                    P.op(P.pe, lambda e: e.matmul(pc[:64, 128:256], lhsT=nl_[:, d0 + 128:d0 + 192], rhs=TRI[:], start=False, stop=True,
                                                  skip_group_check=True), r=nl_.all + TRI.all, w=pc.all)
                    for (pp, cs, ci) in ((128, 0, 0), (64, 128, 1)):
                        P.op(P.act, lambda e: e.activation(out=eq_[:pp, ci, :], in_=pc[:pp, cs:cs + 128], func=AF.Exp, scale=-1.0 / 16),
                             r=pc.all, w=eq_.all)
                        P.op(P.act, lambda e: e.activation(out=ek_[:pp, ci, :], in_=pc[:pp, cs:cs + 128], func=AF.Exp, scale=1.0 / 16),
                             r=pc.all, w=ek_.all)
                    P.op(P.dve, lambda e: e.tensor_tensor(out=qd_[:, 0, :], in0=qa[:, h, :], in1=eq_[:, 0, :], op=ALU.mult),
                         r=qa.all + eq_.all, w=qd_.all)
                    P.op(P.dve, lambda e: e.tensor_tensor(out=qd_[:64, 1, :], in0=qb_[:, h, :], in1=eq_[:64, 1, :], op=ALU.mult),
                         r=qb_.all + eq_.all, w=qd_.all)
                    P.op(P.dve, lambda e: e.tensor_tensor(out=ki_[:, 0, :], in0=ka[:, h, :], in1=ek_[:, 0, :], op=ALU.mult),
                         r=ka.all + ek_.all, w=ki_.all)
                    P.op(P.dve, lambda e: e.tensor_tensor(out=ki_[:64, 1, :], in0=kb_[:, h, :], in1=ek_[:64, 1, :], op=ALU.mult),
                         r=kb_.all + ek_.all, w=ki_.all)
                    pr = P.next_ps()
                    P.op(P.pe, lambda e: e.matmul(pr[:, 0:192], lhsT=STR[:], rhs=nl_[:, d0:d0 + 192], start=True, stop=True),
                         r=nl_.all + STR.all, w=pr.all)
                    P.op(P.act, lambda e: e.activation(out=dkk[:], in_=pr[:, 0:192], func=AF.Exp, scale=-1.0 / 16), r=pr.all, w=dkk.all)
                    pk = P.next_ps()
                    P.op(P.pe, lambda e: e.transpose(pk[:, 0:128], ka[:, h, :], C.ident[:]), r=ka.all + C.ident.all, w=pk.all, sig=False)
                    P.op(P.pe, lambda e: e.transpose(pk[:, 128:192], kb_[:, h, :], C.ident[:64, :64]), r=kb_.all + C.ident.all, w=pk.all)
                    P.op(P.dve, lambda e: e.tensor_tensor(out=kd_[:], in0=pk[:, 0:192], in1=dkk[:], op=ALU.mult),
                         r=pk.all + dkk.all, w=kd_.all)
                    pt = P.next_ps()
                    P.op(P.pe, lambda e: e.matmul(pt[:, 0:2], lhsT=nl_[:, d0:d0 + 128], rhs=IND[:], start=True, stop=True),
                         r=nl_.all + IND.all, w=pt.all, sig=False)
                    P.op(P.pe, lambda e: e.matmul(pt[:64, 2:4], lhsT=nl_[:, d0 + 128:d0 + 192], rhs=IND[:], start=False, stop=True,
                                                  skip_group_check=True), r=nl_.all + IND.all, w=pt.all)
                    P.op(P.act, lambda e: e.activation(out=dec_[:, 0:2], in_=pt[:, 0:2], func=AF.Exp, scale=-1.0 / 16), r=pt.all, w=dec_.all)
                    P.op(P.act, lambda e: e.activation(out=dec_[:64, 2:4], in_=pt[:64, 2:4], func=AF.Exp, scale=-1.0 / 16), r=pt.all, w=dec_.all)
                    pa = P.next_ps()
                    P.op(P.pe, lambda e: e.matmul(pa[:, 0:128], lhsT=ki_[:, 0, :], rhs=qd_[:, 0, :], start=True, stop=False),
                         r=ki_.all + qd_.all, w=pa.all, sig=False)
                    P.op(P.pe, lambda e: e.matmul(pa[:, 0:128], lhsT=ki_[:64, 1, :], rhs=qd_[:64, 1, :], start=False, stop=True),
                         r=ki_.all + qd_.all, w=pa.all)
                    P.op(P.dve, lambda e: e.tensor_tensor(out=am_[:], in0=pa[:, 0:128], in1=TRI[:], op=ALU.mult),
                         r=pa.all + TRI.all, w=am_.all)
                    po = P.next_ps()
                    first = [True]

                    def inter(c):
                        for m in range(3):
                            cs = m * 128 + c * 64
                            P.op(P.pe, lambda e: e.matmul(po[:, cs:cs + 64], lhsT=SAb[h][:, m * 128:(m + 1) * 128],
                                                          rhs=qd_[:, 0, c * 64:(c + 1) * 64], start=False, stop=False, skip_group_check=True),
                                 r=SAb[h].all + qd_.all, w=po.all, sig=False)
                            P.op(P.pe, lambda e: e.matmul(po[:, cs:cs + 64], lhsT=SBb[h][:, m * 128:(m + 1) * 128],
                                                          rhs=qd_[:64, 1, c * 64:(c + 1) * 64], start=False, stop=False, skip_group_check=True),
                                 r=SBb[h].all + qd_.all, w=po.all, sig=(m == 2))

                    def update(c):
                        pva = P.next_ps()
                        pvb = P.next_ps()
                        P.op(P.pe, lambda e: e.matmul(pva[:, 0:384], lhsT=kd_[c * 64:(c + 1) * 64, 0:128],
                                                      rhs=v_[c * 64:(c + 1) * 64, h * 384:(h + 1) * 384], start=True, stop=True),
                             r=kd_.all + v_.all, w=pva.all)
                        P.op(P.pe, lambda e: e.matmul(pvb[:64, 0:384], lhsT=kd_[c * 64:(c + 1) * 64, 128:192],
                                                      rhs=v_[c * 64:(c + 1) * 64, h * 384:(h + 1) * 384], start=True, stop=True),
                             r=kd_.all + v_.all, w=pvb.all)
                        P.op(P.dve, lambda e: e.scalar_tensor_tensor(out=SA[h][:], in0=SA[h][:], scalar=dec_[:, c:c + 1], in1=pva[:, 0:384],
                                                                     op0=ALU.mult, op1=ALU.add), r=SA[h].all + dec_.all + pva.all, w=SA[h].all)
                        P.op(P.dve, lambda e: e.scalar_tensor_tensor(out=SB[h][:], in0=SB[h][:], scalar=dec_[:64, 2 + c:3 + c], in1=pvb[:64, 0:384],
                                                                     op0=ALU.mult, op1=ALU.add), r=SB[h].all + dec_.all + pvb.all, w=SB[h].all)
                        P.op(P.act, lambda e: e.copy(out=SAb[h][:], in_=SA[h][:]), r=SA[h].all, w=SAb[h].all)
                        P.op(P.act, lambda e: e.copy(out=SBb[h][:], in_=SB[h][:]), r=SB[h].all, w=SBb[h].all)

                    for m in range(3):
                        P.op(P.pe, lambda e: e.matmul(po[:, m * 128:(m + 1) * 128], lhsT=v_[:, h * 384 + m * 128:h * 384 + (m + 1) * 128],
                                                      rhs=am_[:], start=(m == 0), stop=False, skip_group_check=True),
                             r=v_.all + am_.all, w=po.all, sig=False)
                    inter(chunks[0])
                    update(chunks[0])
                    inter(chunks[1])
                    update(chunks[1])
                    pov = po[:, 0:384].rearrange("p (m t) -> p m t", t=128)
                    ofd = C.OF.t.ap()[h * 3:(h + 1) * 3, :, t0:t0 + 128].rearrange("m p t -> p m t")
                    if dr == 0:
                        o_ = osum[c2]
                        P.op(P.act, lambda e: e.copy(out=o_[:], in_=pov), r=po.all, w=o_.all)
                        P.dma(P.sp, ofd, o_[:], r=o_.all, w=C.OF.all)
                    else:
                        o_, of_, sq_, sg_, rs_, y_ = osum[c2], ofl[c2], sq[c2], sg[c2], rstd[c2], yb[c2]
                        P.dma(P.sp, of_[:], ofd, r=C.OF.all, w=of_.all)
                        P.dma(P.sp, sg_[:], C.PG.t.ap()[h * 3:(h + 1) * 3, :, t0:t0 + 128].rearrange("m p t -> p m t"), r=C.PG.all, w=sg_.all)
                        P.op(P.dve, lambda e: e.tensor_tensor(out=o_[:], in0=pov, in1=of_[:], op=ALU.add), r=po.all + of_.all, w=o_.all)
                        P.op(P.act, lambda e: e.activation(out=sq_[:], in_=o_[:], func=AF.Square), r=o_.all, w=sq_.all)
                        pss = P.next_ps()
                        for m in range(3):
                            P.op(P.pe, lambda e: e.matmul(pss[:, 0:128], lhsT=C.ones[:], rhs=sq_[:, m, :], start=(m == 0), stop=(m == 2)),
                                 r=sq_.all + C.ones.all, w=pss.all, sig=(m == 2))
                        rstd_from_ss(P, cfg.DV, rs_, pss, 128)
                        for m in range(3):
                            t_ = tt[m % 2]
                            P.op(P.dve, lambda e: e.scalar_tensor_tensor(out=t_[:], in0=o_[:, m, :], scalar=ng[:, h * 3 + m:h * 3 + m + 1],
                                                                         in1=rs_[:], op0=ALU.mult, op1=ALU.mult),
                                 r=o_.all + ng.all + rs_.all, w=t_.all)
                            P.op(P.dve, lambda e: e.tensor_tensor(out=y_[:, m, :], in0=t_[:], in1=sg_[:, m, :], op=ALU.mult),
                                 r=t_.all + sg_.all, w=y_.all)
                        blk = [bb for bb in range(len(cfg.blocks)) if cfg.blocks[bb][0] <= t0 < cfg.blocks[bb][0] + cfg.blocks[bb][1]][0]
                        P.dma(P.sp, C.YB.t.ap()[h * 3:(h + 1) * 3, :, t0:t0 + 128].rearrange("m p t -> p m t"), y_[:], r=y_.all, w=C.YB.r(blk))
    P.barrier()


def na_valid_rows(cfg):
    rows = cfg.ROWS
    kr = min(cfg.KR, rows)
    ws = [min(max(r - cfg.KR // 2, 0), rows - kr) for r in range(rows)]
    return [[r for r in range(rows) if ws[r] <= rp < ws[r] + kr] for rp in range(rows)]


def stage_na(P, C, cfg, l):
    M, T, L = cfg.M, cfg.T, cfg.L
    NT = T // 128
    vq = na_valid_rows(cfg)
    nctx = M // 128
    with contextlib.ExitStack() as st:
        bias = P.sbuf("na_bias", [128, cfg.NH * 15, 64], F32, stack=st)
        rpb = P.sbuf("na_rpb", [31, cfg.NH * 15], F32, stack=st)
        sel = P.sbuf("na_sel", [31, 4096], F32, stack=st)
        bstg = [P.sbuf(f"na_bstg{k}", [128, 512], F32, stack=st) for k in range(2)]
        onesb = P.sbuf("na_ones", [128, 128], BF16, stack=st)
        kT = [P.sbuf(f"na_kT{k}", [128, T], BF16, stack=st) for k in range(2)]
        qT = [P.sbuf(f"na_qT{k}", [128, T], BF16, stack=st) for k in range(2)]
        vv = [P.sbuf(f"na_v{k}", [128, NT, 128], BF16, stack=st) for k in range(2)]
        sb_ = [P.sbuf(f"na_sb{k}", [128, 512], F32, stack=st) for k in range(3)]
        ee = [P.sbuf(f"na_e{k}", [128, 512], BF16, stack=st) for k in range(3)]
        rec = [P.sbuf(f"na_rec{k}", [128, 512], F32, stack=st) for k in range(2)]
        yo = [P.sbuf(f"na_yo{k}", [128, 512], BF16, stack=st) for k in range(2)]
        BT = C.NABT
        P.op(P.dve, lambda e: e.tensor_copy(out=onesb[:], in_=C.ones[:]), r=C.ones.all, w=onesb.all)
        P.dma(P.sp, rpb[:], C.rpbT[l], r=C.rpbT.all, w=rpb.all)
        P.dma(P.sp, sel[:], C.nasel[:], r=C.nasel.all, w=sel.all)
        nrow = cfg.NH * 15
        k = 0
        for m0 in range(0, nrow, 128):
            m = min(128, nrow - m0)
            for n0 in range(0, 4096, 512):
                ps = P.next_ps()
                P.op(P.pe, lambda e: e.matmul(ps[:m, :], lhsT=rpb[:, m0:m0 + m], rhs=sel[:, n0:n0 + 512], start=True, stop=True),
                     r=rpb.all + sel.all, w=ps.all)
                o = bstg[k % 2]
                k += 1
                P.op(P.act, lambda e: e.copy(out=o[:m, :], in_=ps[:m, :]), r=ps.all, w=o.all)
                P.dma(P.sp, BT[m0:m0 + m, n0:n0 + 512], o[:m, :], r=o.all, w=BT.all)
        src = BT.t.ap().rearrange("r (a c) -> a r c", c=64)
        for half in range(2):
            P.dma(P.sp, bias[half * 64:(half + 1) * 64, :, :], src, r=BT.all, w=bias.all)
        cm = C.cst.t[:, C_CMASK:C_CMASK + 64]
        for r_ in range(nrow):
            P.op(P.dve, lambda e: e.tensor_tensor(out=bias[:, r_, :], in0=bias[:, r_, :], in1=cm, op=ALU.add),
                 r=bias.all + C.cst.all, w=bias.all)

        P.ps_rot = 4
        acc = [(P.ps_tiles[4], P.ps_tiles[5]), (P.ps_tiles[6], P.ps_tiles[7])]
        it = 0
        n3 = [0]

        def nxt3():
            n3[0] += 1
            return n3[0] % 3

        for h in range(cfg.NH):
            kt, qt, v = kT[h % 2], qT[h % 2], vv[h % 2]
            P.dma(P.sp, kt[:], C.PNK[h], r=C.PNK.all, w=kt.all)
            P.dma(P.sp, qt[:], C.PNQ[h], r=C.PNQ.all, w=qt.all)
            P.dma(P.sp, v[:], C.PNV.t.ap()[:, h * 128:(h + 1) * 128].rearrange("(n p) d -> p n d", p=128), r=C.PNV.all, w=v.all)
            qblocks = [(M + i * 512, min(512, L - i * 512), i) for i in range((L + 511) // 512)]
            if l != cfg.DEPTH - 1:
                qblocks = [(0, M, -1)] + qblocks
            for (c0, nq, qb) in qblocks:
                po, pd = acc[it % 2]
                it += 1
                for n in range(nctx):
                    ps = P.next_ps()
                    P.op(P.pe, lambda e: e.matmul(ps[:, :nq], lhsT=kt[:, n * 128:(n + 1) * 128], rhs=qt[:, c0:c0 + nq], start=True, stop=True),
                         r=kt.all + qt.all, w=ps.all)
                    e_ = ee[nxt3()]
                    P.op(P.act, lambda e: e.activation(out=e_[:, :nq], in_=ps[:, :nq], func=AF.Exp), r=ps.all, w=e_.all)
                    P.op(P.pe, lambda e: e.matmul(po[:, :nq], lhsT=v[:, n, :], rhs=e_[:, :nq], start=(n == 0), stop=False,
                                                  skip_group_check=True), r=v.all + e_.all, w=po.all)
                    P.op(P.pe, lambda e: e.matmul(pd[:, :nq], lhsT=onesb[:], rhs=e_[:, :nq], start=(n == 0), stop=False,
                                                  skip_group_check=True), r=onesb.all + e_.all, w=pd.all)
                if qb >= 0:
                    r0 = (c0 - M) // 64
                    r1 = r0 + nq // 64
                    for rp in range(cfg.ROWS):
                        qs = [r for r in vq[rp] if r0 <= r < r1]
                        if not qs:
                            continue
                        ra, rb = qs[0], qs[-1]
                        assert qs == list(range(ra, rb + 1))
                        N = (rb - ra + 1) * 64
                        tk = M + rp * 64
                        p0 = tk % 128
                        tn = tk // 128
                        ps = P.next_ps()
                        P.op(P.pe, lambda e: e.matmul(ps[p0:p0 + 64, :N], lhsT=kt[:, tk:tk + 64], rhs=qt[:, M + ra * 64:M + ra * 64 + N],
                                                      start=True, stop=True), r=kt.all + qt.all, w=ps.all)
                        i0 = h * 15 + (7 - rp + ra)
                        bsl = bias[p0:p0 + 64, i0:i0 + (rb - ra + 1), :].rearrange("p a c -> p (a c)")
                        k3 = nxt3()
                        s_ = sb_[k3]
                        e_ = ee[k3]
                        P.op(P.dve, lambda e: e.tensor_tensor(out=s_[p0:p0 + 64, :N], in0=ps[p0:p0 + 64, :N], in1=bsl, op=ALU.add),
                             r=ps.all + bias.all, w=s_.all)
                        P.op(P.act, lambda e: e.activation(out=e_[p0:p0 + 64, :N], in_=s_[p0:p0 + 64, :N], func=AF.Exp), r=s_.all, w=e_.all)
                        cc = (ra - r0) * 64
                        P.op(P.pe, lambda e: e.matmul(po[:, cc:cc + N], lhsT=v[p0:p0 + 64, tn, :], rhs=e_[p0:p0 + 64, :N], start=False, stop=False,
                                                      skip_group_check=True), r=v.all + e_.all, w=po.all)
                        P.op(P.pe, lambda e: e.matmul(pd[:, cc:cc + N], lhsT=onesb[p0:p0 + 64, :], rhs=e_[p0:p0 + 64, :N], start=False, stop=False,
                                                      skip_group_check=True), r=onesb.all + e_.all, w=pd.all)
                rc_ = rec[it % 2]
                y_ = yo[it % 2]
                P.op(P.dve, lambda e: e.reciprocal(out=rc_[:, :nq], in_=pd[:, :nq]), r=pd.all, w=rc_.all)
                P.op(P.dve, lambda e: e.tensor_tensor(out=y_[:, :nq], in0=po[:, :nq], in1=rc_[:, :nq], op=ALU.mult),
                     r=po.all + rc_.all, w=y_.all)
                blks = [bb for bb in range(len(cfg.blocks)) if cfg.blocks[bb][0] == c0]
                P.dma(P.sp, C.YC[h, :, c0:c0 + nq], y_[:, :nq], r=y_.all, w=C.YC.r(blks[0]))
        P.ps_rot = 8
    P.barrier()


def fm_vec(v, kc):
    v = np.asarray(v, np.float32)
    return np.ascontiguousarray(np.swapaxes(v.reshape(v.shape[:-1] + (kc, 128)), -1, -2))


def rope_tables(cfg):
    n = 48
    freq = (10000.0 ** (-np.arange(n, dtype=np.float32) / n)).astype(np.float32)
    pos = np.arange(cfg.L)
    ang_r = (pos // cfg.GW).astype(np.float32)[None, :] * freq[:, None]
    ang_c = (pos % cfg.GW).astype(np.float32)[None, :] * freq[:, None]
    cos = np.concatenate([np.cos(ang_r), np.cos(ang_r), np.cos(ang_c), np.cos(ang_c)], 0).astype(np.float32)
    sin = np.concatenate([-np.sin(ang_r), np.sin(ang_r), -np.sin(ang_c), np.sin(ang_c)], 0).astype(np.float32)
    qs = np.float32(cfg.DK ** -0.5)
    return np.ascontiguousarray(np.stack([cos * qs, sin * qs, cos, sin]).astype(np.float32))


def na_sel():
    sel = np.zeros((31, 64, 64), np.float32)
    for kc_ in range(64):
        for qc_ in range(64):
            d = kc_ - qc_ + 15
            if 0 <= d <= 30:
                sel[d, kc_, qc_] = 1.0
    return sel.reshape(31, 4096)


def shared_inputs(cfg, inp):
    KC = cfg.KCD
    dep = cfg.DEPTH
    sh = {}
    sh["mod_a"] = np.ascontiguousarray(inp["mod_a"], np.float32)
    sh["mod_b"] = np.ascontiguousarray(inp["mod_b"], np.float32)
    sh["mod_bias"] = fm_vec(np.asarray(inp["mod_bias"]).reshape(dep, 9, cfg.D), KC).transpose(0, 2, 1, 3).copy()
    sh["norm_g"] = fm_vec(inp["norm_g"], KC).transpose(0, 2, 1, 3).copy()
    sh["ffn_up"] = np.ascontiguousarray(inp["ffn_up"], np.float32)
    sh["ffn_down"] = np.ascontiguousarray(inp["ffn_down"], np.float32)
    sh["w_in"] = np.ascontiguousarray(inp["w_in"], np.float32)
    cw = np.asarray(inp["conv_w"], np.float32)
    sh["conv_w"] = np.ascontiguousarray(fm_vec(cw, cfg.CW // 128).transpose(0, 2, 3, 1))
    dw = np.asarray(inp["gla_decay_w"], np.float32)
    db = np.asarray(inp["gla_decay_b"], np.float32)[:, :, None, :]
    sh["decay_w"] = np.ascontiguousarray(np.concatenate([dw, db], axis=2))
    sh["gla_ng"] = fm_vec(inp["gla_norm_g"], cfg.GV // 128)
    rp = np.asarray(inp["na_rpb"], np.float32)[:, :, ::-1, :].reshape(dep, cfg.NH * 15, 31)
    sh["rpbT"] = np.ascontiguousarray(rp.transpose(0, 2, 1))
    sh["wb_conv"] = np.ascontiguousarray(inp["w_branch_conv"], np.float32)
    sh["wb_gla"] = np.ascontiguousarray(inp["w_branch_gla"], np.float32)
    sh["wb_na"] = np.ascontiguousarray(inp["w_branch_na"], np.float32)
    sh["w_out"] = np.ascontiguousarray(inp["w_out"], np.float32)
    sh["final_g"] = fm_vec(inp["final_g"], KC)
    sh["consts"] = host_consts()
    sh["rope"] = rope_tables(cfg)
    sh["nasel"] = na_sel()
    return sh


def core_inputs(cfg, inp, sh, b):
    m = dict(sh)
    m["x_in"] = np.ascontiguousarray(inp["x"][b], np.float32)
    m["ctx_in"] = np.ascontiguousarray(inp["ctx"][b], np.float32)
    cv = np.stack([np.asarray(inp["c"][b], np.float32), np.asarray(inp["c_ctx"], np.float32)], 0)
    m["cvec"] = np.ascontiguousarray(fm_vec(cv, cfg.KCD).transpose(1, 2, 0))
    return m


_CACHE = {}


def kernel(**inputs):
    cfg = Cfg()
    if "nc" not in _CACHE:
        _CACHE["nc"] = build(cfg)[0]
    nc = _CACHE["nc"]
    sh = shared_inputs(cfg, inputs)
    nb = inputs["x"].shape[0]
    in_maps = [core_inputs(cfg, inputs, sh, b) for b in range(nb)]
    res = run_bass_kernel_spmd(nc, in_maps, core_ids=list(range(nb)))
    return np.stack([np.asarray(r["out"], np.float32) for r in res.results], 0)
```

```python
import contextlib
import math
import numpy as np
import concourse.bass as bass
import concourse.mybir as mybir
from concourse.bass_utils import run_bass_kernel_spmd

F32 = mybir.dt.float32
BF16 = mybir.dt.bfloat16
AF = mybir.ActivationFunctionType
ALU = mybir.AluOpType
AX = mybir.AxisListType

EPS = 1e-6


class Region:
    __slots__ = ("name", "last_w", "readers")

    def __init__(self, name=""):
        self.name = name
        self.last_w = None
        self.readers = []


class Tile:
    def __init__(self, t, nreg=1, name=""):
        self.t = t
        self.regs = [Region(f"{name}.{i}") for i in range(nreg)]

    def __getitem__(self, k):
        return self.t[k]

    @property
    def all(self):
        return self.regs

    def r(self, i):
        return [self.regs[i]]


class Eng:
    def __init__(self, P, name, e):
        self.P = P
        self.name = name
        self.e = e
        self.sem = None
        self.count = 0
        self.waited = {}
        self.pend_r = []
        self.pend_w = []

    def new_sem(self):
        self.sem = self.P.alloc_sem(f"s_{self.name}_{self.P.nsem}")
        self.count = 0


SEM_ROLL = 30000


class Prog:
    def __init__(self, nc):
        self.nc = nc
        self.es = contextlib.ExitStack()
        self.nsem = 0
        self.pe = Eng(self, "pe", nc.tensor)
        self.act = Eng(self, "act", nc.scalar)
        self.dve = Eng(self, "dve", nc.vector)
        self.pool = Eng(self, "pool", nc.gpsimd)
        self.sp = Eng(self, "sp", nc.sync)
        self.engs = [self.pe, self.act, self.dve, self.pool, self.sp]
        self.all_sems = []
        for E in self.engs:
            E.new_sem()
        self.dsems = [[self.alloc_sem(f"dma{i}"), 0] for i in range(24)]
        self.dnext = 0
        self.qsems = [[self.alloc_sem(f"swdma{i}"), 0] for i in range(8)]
        self.qnext = 0
        self.n_ins = 0
        self.ps_tiles = None
        self.ps_next = 0
        self.ev_flip = 0

    def alloc_sem(self, name):
        self.nsem += 1
        return self.es.enter_context(self.nc.semaphore(name))

    def sbuf(self, name, shape, dt, nreg=1, stack=None):
        self.uid = getattr(self, "uid", 0) + 1
        name = f"sb{self.uid}_{name}"
        t = (stack or self.es).enter_context(self.nc.sbuf_tensor(name, list(shape), dt))
        return Tile(t, nreg, name)

    def psum(self, name, shape, dt, nreg=1):
        t = self.es.enter_context(self.nc.psum_tensor(name, list(shape), dt))
        return Tile(t, nreg, name)

    def dram(self, name, shape, dt, kind="Internal", nreg=1):
        t = self.nc.dram_tensor(name, list(shape), dt, kind=kind)
        return Tile(t, nreg, name)

    def next_ps(self):
        n = getattr(self, "ps_rot", 8)
        self.ps_next = (self.ps_next + 1) % n
        return self.ps_tiles[self.ps_next]

    def _waits(self, E, r, w):
        need = {}

        def add(ev):
            if ev is None:
                return
            s, v = ev
            k = id(s)
            if k not in need or need[k][1] < v:
                need[k] = (s, v)

        for reg in r:
            add(reg.last_w)
        for reg in w:
            add(reg.last_w)
            for ev in reg.readers:
                add(ev)
        for k, (s, v) in need.items():
            if E.waited.get(k, 0) >= v:
                continue
            if s is E.sem and v > E.count:
                continue
            E.e.wait_ge(s, v)
            E.waited[k] = v

    def _commit(self, ev, r, w):
        for reg in r:
            reg.readers.append(ev)
            if len(reg.readers) > 64:
                best = {}
                for s, v in reg.readers:
                    if id(s) not in best or best[id(s)][1] < v:
                        best[id(s)] = (s, v)
                reg.readers = list(best.values())
        for reg in w:
            reg.last_w = ev
            reg.readers = []

    def op(self, E, fn, r=(), w=(), sig=True):
        r = list(r)
        w = list(w)
        if E.count >= SEM_ROLL and not E.pend_r and not E.pend_w:
            E.new_sem()
        self._waits(E, r, w)
        ins = fn(E.e)
        self.n_ins += 1
        if sig:
            E.count += 1
            ins.then_inc(E.sem, 1)
            ev = (E.sem, E.count)
            self._commit(ev, r + E.pend_r, w + E.pend_w)
            E.pend_r = []
            E.pend_w = []
        else:
            E.pend_r += r
            E.pend_w += w
            self._commit((E.sem, E.count + 1), r, w)
        return ins

    def dma(self, E, out, in_, r=(), w=(), **kw):
        r = list(r)
        w = list(w)
        if E is self.pool:
            slot = self.qsems[self.qnext]
            self.qnext = (self.qnext + 1) % len(self.qsems)
        else:
            slot = self.dsems[self.dnext]
            self.dnext = (self.dnext + 1) % len(self.dsems)
        s, tot = slot
        k = id(s)
        if tot > 0 and E.waited.get(k, 0) < tot:
            E.e.wait_ge(s, tot)
            E.waited[k] = tot
        self._waits(E, r, w)
        ins = E.e.dma_start(out=out, in_=in_, **kw)
        ins.then_inc(s, 16)
        slot[1] = tot + 16
        self.n_ins += 1
        self._commit((s, tot + 16), r, w)

    def barrier(self):
        for X in self.engs:
            assert not X.pend_r and not X.pend_w
        for E in self.engs:
            for X in self.engs:
                if X is E or X.count == 0:
                    continue
                k = id(X.sem)
                if E.waited.get(k, 0) < X.count:
                    E.e.wait_ge(X.sem, X.count)
                    E.waited[k] = X.count
            for s, tot in self.dsems + self.qsems:
                if tot > 0 and E.waited.get(id(s), 0) < tot:
                    E.e.wait_ge(s, tot)
                    E.waited[id(s)] = tot

    def ev_eng(self):
        self.ev_flip ^= 1
        return self.act if self.ev_flip else self.dve


class Cfg:
    def __init__(self, D=4096, FF=3072, CW=1024, ROWS=32, M=256, DEPTH=4, RANK=256):
        self.D, self.FF, self.CW, self.ROWS, self.M, self.DEPTH, self.RANK = D, FF, CW, ROWS, M, DEPTH, RANK
        self.GW = 64
        self.L = ROWS * 64
        self.T = self.M + self.L
        self.GH, self.DK, self.DV = 4, 192, 384
        self.NH, self.HD = 12, 128
        self.KR, self.KC = 8, 16
        self.KCD = D // 128
        self.GK = self.GH * self.DK
        self.GV = self.GH * self.DV
        self.NW = self.NH * self.HD
        sizes = (CW, CW, CW, self.GK, self.GK, self.GV, self.GV, 32, self.NW, self.NW, self.NW, D, D, D)
        offs = np.concatenate([[0], np.cumsum(sizes)])
        names = ["h", "bg", "cg", "q", "k", "v", "g", "lr", "nq", "nk", "nv", "ga", "gb", "gc"]
        self.off = {n: int(o) for n, o in zip(names, offs[:-1])}
        self.NIN = int(offs[-1])
        self.blocks = [(0, self.M, 1)] + [(self.M + i * 512, min(512, self.L - i * 512), 0)
                                          for i in range((self.L + 511) // 512)]
        nb = len(self.blocks)
        self.border = ([1, 0] + list(range(2, nb))) if nb > 1 else list(range(nb))


USE_WCACHE = True
WCOLS = 256
WBUF_ELEMS = 32 * 256


class Ctx:
    pass


def w_view(Wt, row0, nrows, col0, ncols):
    ap = Wt[row0:row0 + nrows, col0:col0 + ncols]
    if nrows <= 128:
        return ap
    return ap.rearrange("(kc p) n -> p kc n", p=128)


NSLOT = 112


def load_w(P, C, Wt, wreg, row0, nrows, col0, ncols, dst_col=0, wt=None, perm48=False, key="auto"):
    if wt is None:
        wt = C.wbufs[C.wnext]
        C.wnext = (C.wnext + 1) % len(C.wbufs)
    kc = max(1, nrows // 128)
    pp = min(nrows, 128)
    view = wt.t[:pp, 0:kc * WCOLS].rearrange("p (kc c) -> p kc c", c=WCOLS)
    cache = getattr(C, "wcache", None)
    if key == "auto":
        key = (id(wreg[0]), row0, nrows, col0, ncols, perm48, dst_col)
    slot = None
    if cache is not None and key is not None:
        slot = C.wslots.get(key)
        if slot is not None:
            cv = cache.t[slot, :pp, 0:kc * WCOLS].rearrange("p (kc c) -> p kc c", c=WCOLS)
            P.dma(P.pool, view[:, :, 0:ncols], cv[:, :, 0:ncols], r=cache.r(slot), w=wt.all)
            return wt, view
    if not perm48:
        src = w_view(Wt, row0, nrows, col0, ncols)
        if nrows <= 128:
            P.dma(P.pool, view[:, 0, dst_col:dst_col + ncols], src, r=wreg, w=wt.all)
        else:
            P.dma(P.pool, view[:, :, dst_col:dst_col + ncols], src, r=wreg, w=wt.all)
    else:
        for b in range(4):
            sb = b + 1 if b % 2 == 0 else b - 1
            src = w_view(Wt, row0, nrows, col0 + sb * 48, 48)
            P.dma(P.pool, view[:, :, b * 48:(b + 1) * 48], src, r=wreg, w=wt.all)
    if cache is not None and key is not None and len(C.wslots) < NSLOT and dst_col == 0:
        slot = len(C.wslots)
        C.wslots[key] = slot
        cv = cache.t[slot, :pp, 0:kc * WCOLS].rearrange("p (kc c) -> p kc c", c=WCOLS)
        P.dma(P.pool, cv[:, :, 0:ncols], view[:, :, 0:ncols], r=wt.all, w=cache.r(slot))
    return wt, view


def stream_fm(P, C, Wt, wreg, K, col0, ncols, act, act_regs, nt, evac, group=WCOLS, chunk=128, perm48=False):
    KC = max(1, K // 128)
    ci = 0
    for g0 in range(0, ncols, group):
        gc = min(group, ncols - g0)
        wt, view = load_w(P, C, Wt, wreg, 0, K, col0 + g0, gc, perm48=perm48)
        for j0 in range(0, gc, chunk):
            m = min(chunk, gc - j0)
            ps = P.next_ps()
            for kc in range(KC):
                kp = min(128, K - kc * 128)
                P.op(P.pe, lambda e: e.matmul(ps[:m, :nt], lhsT=view[:kp, kc, j0:j0 + m], rhs=act[:kp, kc, :nt],
                                              start=(kc == 0), stop=(kc == KC - 1)),
                     r=wt.all + act_regs, w=ps.all, sig=(kc == KC - 1))
            evac(ci, g0 + j0, m, ps)
            ci += 1


def stream_tm(P, C, Wt, wreg, K, col0, ncols, act, act_regs, nt, evac):
    KC = K // 128
    for g0 in range(0, ncols, WCOLS):
        gc = min(WCOLS, ncols - g0)
        wt, view = load_w(P, C, Wt, wreg, 0, K, col0 + g0, gc)
        for ti in range(nt // 128):
            ps = P.next_ps()
            for kc in range(KC):
                P.op(P.pe, lambda e: e.matmul(ps[:, :gc], lhsT=act[:, kc, ti * 128:(ti + 1) * 128], rhs=view[:, kc, :gc],
                                              start=(kc == 0), stop=(kc == KC - 1)),
                     r=wt.all + act_regs, w=ps.all, sig=(kc == KC - 1))
            evac(ti, g0, gc, ps)


def rstd_from_ss(P, n, rstd, ps, nt, pp=128):
    P.op(P.dve, lambda e: e.tensor_scalar(out=rstd[:pp, :nt], in0=ps[:pp, :nt], scalar1=1.0 / n, scalar2=EPS,
                                          op0=ALU.mult, op1=ALU.add), r=ps.all, w=rstd.all)
    P.op(P.act, lambda e: e.activation(out=rstd[:pp, :nt], in_=rstd[:pp, :nt], func=AF.Sqrt), r=rstd.all, w=rstd.all)
    P.op(P.dve, lambda e: e.reciprocal(out=rstd[:pp, :nt], in_=rstd[:pp, :nt]), r=rstd.all, w=rstd.all)


def stage_init(P, C, cfg):
    with contextlib.ExitStack() as st:
        xin = [P.sbuf(f"xin{i}", [128, cfg.D], F32, stack=st) for i in range(2)]
        xo = [P.sbuf(f"xo{i}", [128, cfg.KCD, 128], F32, stack=st) for i in range(2)]
        for ti in range(cfg.T // 128):
            t0 = ti * 128
            a = xin[ti % 2]
            o = xo[ti % 2]
            if t0 < cfg.M:
                P.dma(P.sp, a[:], C.ctx_in[t0:t0 + 128, :], r=C.ctx_in.all, w=a.all)
            else:
                P.dma(P.sp, a[:], C.x_in[t0 - cfg.M:t0 - cfg.M + 128, :], r=C.x_in.all, w=a.all)
            for k4 in range(0, cfg.KCD, 4):
                ps = P.next_ps()
                n4 = min(4, cfg.KCD - k4)
                for q in range(n4):
                    kc = k4 + q
                    P.op(P.pe, lambda e: e.transpose(ps[:, q * 128:(q + 1) * 128], a[:, kc * 128:(kc + 1) * 128], C.ident[:]),
                         r=a.all + C.ident.all, w=ps.all, sig=(q == n4 - 1))
                E = P.ev_eng()
                src = ps[:, 0:n4 * 128].rearrange("p (k c) -> p k c", c=128)
                if E is P.act:
                    P.op(E, lambda e: e.copy(out=o[:, k4:k4 + n4, :], in_=src), r=ps.all, w=o.all)
                else:
                    P.op(E, lambda e: e.tensor_copy(out=o[:, k4:k4 + n4, :], in_=src), r=ps.all, w=o.all)
            P.dma(P.sp, C.HT.t.ap()[:, :, t0:t0 + 128].rearrange("k p t -> p k t"), o[:], r=o.all, w=C.HT.all)
    P.barrier()


def stage_final(P, C, cfg):
    with contextlib.ExitStack() as st:
        KC = cfg.KCD
        hb = P.sbuf("fin_h", [128, KC, 128], F32, stack=st)
        sq = [P.sbuf(f"fin_sq{i}", [128, 128], F32, stack=st) for i in range(2)]
        rstd = P.sbuf("fin_rstd", [128, 128], F32, stack=st)
        yn = [P.sbuf(f"fin_y{i}", [128, 128], F32, stack=st) for i in range(2)]
        ot = [P.sbuf(f"fin_o{i}", [128, cfg.D], F32, stack=st) for i in range(2)]
        for ti in range(cfg.L // 128):
            t0 = cfg.M + ti * 128
            P.dma(P.sp, hb[:], C.HT.t.ap()[:, :, t0:t0 + 128].rearrange("k p t -> p k t"), r=C.HT.all, w=hb.all)
            ps = P.next_ps()
            for kc in range(KC):
                s = sq[kc % 2]
                P.op(P.act, lambda e: e.activation(out=s[:], in_=hb[:, kc, :], func=AF.Square), r=hb.all, w=s.all)
                P.op(P.pe, lambda e: e.matmul(ps[:, :128], lhsT=C.ones[:], rhs=s[:], start=(kc == 0), stop=(kc == KC - 1)),
                     r=s.all + C.ones.all, w=ps.all, sig=True)
            rstd_from_ss(P, cfg.D, rstd, ps, 128)
            o = ot[ti % 2]
            for k4 in range(0, KC, 4):
                n4 = min(4, KC - k4)
                ps2 = P.next_ps()
                for q in range(n4):
                    kc = k4 + q
                    y = yn[kc % 2]
                    P.op(P.dve, lambda e: e.scalar_tensor_tensor(out=y[:], in0=hb[:, kc, :], scalar=C.fing[:, kc:kc + 1],
                                                                 in1=rstd[:], op0=ALU.mult, op1=ALU.mult),
                         r=hb.all + rstd.all + C.fing.all, w=y.all)
                    P.op(P.pe, lambda e: e.transpose(ps2[:, q * 128:(q + 1) * 128], y[:], C.ident[:]),
                         r=y.all + C.ident.all, w=ps2.all, sig=True)
                P.op(P.act, lambda e: e.copy(out=o[:, k4 * 128:(k4 + n4) * 128], in_=ps2[:, 0:n4 * 128]), r=ps2.all, w=o.all)
            P.dma(P.sp, C.out[ti * 128:(ti + 1) * 128, :], o[:], r=o.all, w=C.out.all)
    P.barrier()


def stage_mod(P, C, cfg, l):
    D, KC, R = cfg.D, cfg.KCD, cfg.RANK
    RC = R // 128
    with contextlib.ExitStack() as st:
        rT = P.sbuf("mod_rT", [128, RC, 2], BF16, stack=st)
        bias = P.sbuf("mod_bias", [128, 9, KC], F32, stack=st)
        ng = P.sbuf("mod_ng", [128, 3, KC], F32, stack=st)
        P.dma(P.sp, bias[:], C.mod_bias[l], r=C.mod_bias.all, w=bias.all)
        P.dma(P.sp, ng[:], C.norm_g[l], r=C.norm_g.all, w=ng.all)
        wA = []
        for rc in range(RC):
            wt, view = load_w(P, C, C.mod_a.t[l], C.mod_a.all, 0, D, rc * 128, 128, key=None)
            wA.append((wt, view))
        for rc in range(RC):
            wt, view = wA[rc]
            ps = P.next_ps()
            for kc in range(KC):
                P.op(P.pe, lambda e: e.matmul(ps[:, 0:2], lhsT=view[:, kc, 0:128], rhs=C.scb[:, kc, :],
                                              start=(kc == 0), stop=(kc == KC - 1)),
                     r=wt.all + C.scb.all, w=ps.all, sig=(kc == KC - 1))
            P.op(P.dve, lambda e: e.tensor_copy(out=rT[:, rc, :], in_=ps[:, 0:2]), r=ps.all, w=rT.all)
        ncols = 9 * D
        for g0 in range(0, ncols, WCOLS):
            wt, view = load_w(P, C, C.mod_b.t[l], C.mod_b.all, 0, R, g0, WCOLS, key=None)
            ps = P.next_ps()
            nch = WCOLS // 128
            for j in range(nch):
                for rc in range(RC):
                    P.op(P.pe, lambda e: e.matmul(ps[:, 2 * j:2 * j + 2], lhsT=view[:, rc, j * 128:(j + 1) * 128], rhs=rT[:, rc, :],
                                                  start=(rc == 0 and j == 0), stop=(rc == RC - 1),
                                                  skip_group_check=True),
                         r=wt.all + rT.all, w=ps.all, sig=(rc == RC - 1 and j == nch - 1))
            idx0 = g0 // 128
            dst = C.modt.t[:, :, :, :].rearrange("p j k c -> p (j k) c")[:, idx0:idx0 + nch, :]
            srcp = ps[:, 0:2 * nch].rearrange("p (a c) -> p a c", c=2)
            P.op(P.dve, lambda e: e.tensor_copy(out=dst, in_=srcp), r=ps.all, w=C.modt.all)
        for kind in range(2):
            P.op(P.dve, lambda e: e.tensor_tensor(out=C.modt[:, :, :, kind], in0=C.modt[:, :, :, kind], in1=bias[:],
                                                  op=ALU.add), r=C.modt.all + bias.all, w=C.modt.all)
        for i in range(3):
            for kind in range(2):
                P.op(P.dve, lambda e: e.scalar_tensor_tensor(out=C.Gs[:, i, :, kind], in0=C.modt[:, 3 * i + 1, :, kind],
                                                             scalar=1.0, in1=ng[:, i, :], op0=ALU.add, op1=ALU.mult),
                     r=C.modt.all + ng.all, w=C.Gs.all)
                P.op(P.dve, lambda e: e.tensor_copy(out=C.SHs[:, i, :, kind], in_=C.modt[:, 3 * i, :, kind]),
                     r=C.modt.all, w=C.SHs.all)
                P.op(P.dve, lambda e: e.tensor_scalar(out=C.GTs[:, i, :, kind], in0=C.modt[:, 3 * i + 2, :, kind],
                                                      scalar1=(1.0 if i == 1 else 0.5), scalar2=None, op0=ALU.mult),
                     r=C.modt.all, w=C.GTs.all)
    P.barrier()


def norm_block(P, C, cfg, st_tiles, i, t0, nt, kind, uT, blk):
    KC = cfg.KCD
    hbuf, sq, rstd, tmp = st_tiles
    hreg = C.HT.r(blk)
    ps = P.next_ps()
    for kc in range(KC):
        hc = hbuf[kc % len(hbuf)]
        P.dma(P.sp, hc[:, :nt], C.HT[kc, :, t0:t0 + nt], r=hreg, w=hc.all)
        s = sq[kc % 2]
        P.op(P.act, lambda e: e.activation(out=s[:, :nt], in_=hc[:, :nt], func=AF.Square), r=hc.all, w=s.all)
        P.op(P.pe, lambda e: e.matmul(ps[:, :nt], lhsT=C.ones[:], rhs=s[:, :nt], start=(kc == 0), stop=(kc == KC - 1)),
             r=s.all + C.ones.all, w=ps.all, sig=True)
    rstd_from_ss(P, cfg.D, rstd, ps, nt)
    for kc in range(KC):
        hc = hbuf[kc % len(hbuf)]
        P.dma(P.sp, hc[:, :nt], C.HT[kc, :, t0:t0 + nt], r=hreg, w=hc.all)
        tt = tmp[kc % 2]
        P.op(P.dve, lambda e: e.scalar_tensor_tensor(out=tt[:, :nt], in0=hc[:, :nt], scalar=C.Gs[:, i, kc, kind:kind + 1],
                                                     in1=rstd[:, :nt], op0=ALU.mult, op1=ALU.mult),
             r=hc.all + rstd.all + C.Gs.all, w=tt.all)
        P.op(P.act, lambda e: e.activation(out=uT[:, kc, :nt], in_=tt[:, :nt], func=AF.Identity,
                                           bias=C.SHs[:, i, kc, kind:kind + 1], scale=1.0),
             r=tt.all + C.SHs.all, w=uT.all)


def alloc_norm_tiles(P, st, pfx):
    hbuf = [P.sbuf(f"{pfx}_h{i}", [128, 512], F32, stack=st) for i in range(3)]
    sq = [P.sbuf(f"{pfx}_sq{i}", [128, 512], F32, stack=st) for i in range(2)]
    rstd = P.sbuf(f"{pfx}_rstd", [128, 512], F32, stack=st)
    tmp = [P.sbuf(f"{pfx}_tmp{i}", [128, 512], F32, stack=st) for i in range(2)]
    return hbuf, sq, rstd, tmp


def stage_ffn(P, C, cfg, l, which):
    i = 0 if which == 0 else 2
    D, FF, KC = cfg.D, cfg.FF, cfg.KCD
    FC = FF // 128
    Wup = C.ffn_up.t[l, which]
    Wdn = C.ffn_down.t[l, which]
    C.wslots = {}
    with contextlib.ExitStack() as st:
        nt_tiles = alloc_norm_tiles(P, st, "ffn")
        uT = P.sbuf("ffn_uT", [128, KC, 512], BF16, stack=st)
        hm = P.sbuf("ffn_hm", [128, FC, 512], BF16, stack=st)
        sa = [P.sbuf(f"ffn_sa{k}", [128, 512], F32, stack=st) for k in range(2)]
        hres = [P.sbuf(f"ffn_hr{k}", [128, 512], F32, stack=st) for k in range(2)]
        hnew = [P.sbuf(f"ffn_hn{k}", [128, 512], F32, stack=st) for k in range(2)]
        for blk, (t0, nt, kind) in ((b_, cfg.blocks[b_]) for b_ in cfg.border):
            if kind == 1 and l == cfg.DEPTH - 1 and which == 1:
                continue
            norm_block(P, C, cfg, nt_tiles, i, t0, nt, kind, uT, blk)
            n = 0
            for g0 in range(0, FF, WCOLS):
                wa, va = load_w(P, C, Wup, C.ffn_up.all, 0, D, g0, WCOLS)
                wb, vb = load_w(P, C, Wup, C.ffn_up.all, 0, D, FF + g0, WCOLS)
                for j in range(WCOLS // 128):
                    psa = P.next_ps()
                    psb = P.next_ps()
                    for kc in range(KC):
                        P.op(P.pe, lambda e: e.matmul(psa[:, :nt], lhsT=va[:, kc, j * 128:(j + 1) * 128], rhs=uT[:, kc, :nt],
                                                      start=(kc == 0), stop=(kc == KC - 1)),
                             r=wa.all + uT.all, w=psa.all, sig=(kc == KC - 1))
                    for kc in range(KC):
                        P.op(P.pe, lambda e: e.matmul(psb[:, :nt], lhsT=vb[:, kc, j * 128:(j + 1) * 128], rhs=uT[:, kc, :nt],
                                                      start=(kc == 0), stop=(kc == KC - 1)),
                             r=wb.all + uT.all, w=psb.all, sig=(kc == KC - 1))
                    s = sa[n % 2]
                    fc = (g0 // 128) + j
                    P.op(P.act, lambda e: e.activation(out=s[:, :nt], in_=psa[:, :nt], func=AF.Silu), r=psa.all, w=s.all)
                    P.op(P.dve, lambda e: e.tensor_tensor(out=hm[:, fc, :nt], in0=s[:, :nt], in1=psb[:, :nt], op=ALU.mult),
                         r=s.all + psb.all, w=hm.all)
                    n += 1
            hreg = C.HT.r(blk)

            def evac(ci, coff, m, ps):
                kc = coff // 128
                hr = hres[ci % 2]
                hn = hnew[ci % 2]
                P.dma(P.sp, hr[:, :nt], C.HT[kc, :, t0:t0 + nt], r=hreg, w=hr.all)
                P.op(P.dve, lambda e: e.scalar_tensor_tensor(out=hn[:, :nt], in0=ps[:, :nt], scalar=C.GTs[:, i, kc, kind:kind + 1],
                                                             in1=hr[:, :nt], op0=ALU.mult, op1=ALU.add),
                     r=ps.all + hr.all + C.GTs.all, w=hn.all)
                P.dma(P.sp, C.HT[kc, :, t0:t0 + nt], hn[:, :nt], r=hn.all, w=hreg)

            stream_fm(P, C, Wdn, C.ffn_down.all, FF, 0, D, hm, hm.all, nt, evac)
    P.barrier()


def declare_io(P, cfg, C):
    D, KC = cfg.D, cfg.KCD
    ext = lambda n, s: P.dram(n, s, F32, kind="ExternalInput")
    C.x_in = ext("x_in", [cfg.L, D])
    C.ctx_in = ext("ctx_in", [cfg.M, D])
    C.cvec = ext("cvec", [128, KC, 2])
    C.mod_a = ext("mod_a", [cfg.DEPTH, D, cfg.RANK])
    C.mod_b = ext("mod_b", [cfg.DEPTH, cfg.RANK, 9 * D])
    C.mod_bias = ext("mod_bias", [cfg.DEPTH, 128, 9, KC])
    C.norm_g = ext("norm_g", [cfg.DEPTH, 128, 3, KC])
    C.ffn_up = ext("ffn_up", [cfg.DEPTH, 2, D, 2 * cfg.FF])
    C.ffn_down = ext("ffn_down", [cfg.DEPTH, 2, cfg.FF, D])
    C.w_in = ext("w_in", [cfg.DEPTH, D, cfg.NIN])
    C.conv_w = ext("conv_w", [cfg.DEPTH, 128, cfg.CW // 128, 3])
    C.decay_w = ext("decay_w", [cfg.DEPTH, 2, 17, cfg.GK])
    C.gla_ng = ext("gla_ng", [cfg.DEPTH, 128, cfg.GV // 128])
    C.rpbT = ext("rpbT", [cfg.DEPTH, 31, cfg.NH * 15])
    C.wb_conv = ext("wb_conv", [cfg.DEPTH, cfg.CW, D])
    C.wb_gla = ext("wb_gla", [cfg.DEPTH, cfg.GV, D])
    C.wb_na = ext("wb_na", [cfg.DEPTH, cfg.NW, D])
    C.w_out = ext("w_out", [cfg.DEPTH, D, D])
    C.fing_in = ext("final_g", [128, KC])
    C.consts = ext("consts", [128, NCONST])
    C.rope = ext("rope", [4, 192, cfg.L])
    C.nasel = ext("nasel", [31, 4096])
    C.out = P.dram("out", [cfg.L, D], F32, kind="ExternalOutput")


C_IDENT, C_ONES, C_TRIF, C_TRIB, C_STRF, C_STRB, C_IND, C_CMASK = 0, 128, 256, 384, 512, 640, 768, 770
NCONST = 770 + 64


def host_consts():
    c = np.zeros((128, NCONST), np.float32)
    c[:, C_IDENT:C_IDENT + 128] = np.eye(128)
    c[:, C_ONES:C_ONES + 128] = 1.0
    s = np.arange(128)[:, None]
    t = np.arange(128)[None, :]
    same = (s // 64) == (t // 64)
    c[:, C_TRIF:C_TRIF + 128] = (same & (s <= t))
    c[:, C_TRIB:C_TRIB + 128] = (same & (s >= t))
    c[:, C_STRF:C_STRF + 128] = (same & (s > t))
    c[:, C_STRB:C_STRB + 128] = (same & (s < t))
    c[:, C_IND] = (np.arange(128) < 64)
    c[:, C_IND + 1] = (np.arange(128) >= 64)
    kc_ = np.arange(64)[:, None]
    qc_ = np.arange(64)[None, :]
    win = np.clip(qc_ - 8, 0, 48)
    valid = (kc_ >= win) & (kc_ < win + 16)
    c[:64, C_CMASK:C_CMASK + 64] = np.where(valid, 0.0, -30000.0)
    c[64:, C_CMASK:C_CMASK + 64] = np.where(valid, 0.0, -30000.0)
    return c


def build(cfg, n_layers=None, stages=("ffn0", "mix", "ffn1")):
    nc = bass.Bass("TRN2", target_bir_lowering=False)
    P = Prog(nc)
    C = Ctx()
    declare_io(P, cfg, C)
    D, KC = cfg.D, cfg.KCD
    NB = len(cfg.blocks)
    C.HT = P.dram("HT", [KC, 128, cfg.T], F32, nreg=NB)
    P.ps_tiles = [P.psum(f"ps{i}", [128, 512], F32) for i in range(8)]
    C.wbufs = [P.sbuf(f"wbuf{i}", [128, WBUF_ELEMS], BF16) for i in range(4)]
    C.wnext = 0
    C.wslots = {}
    if USE_WCACHE:
        C.wcache = P.dram("wcache", [NSLOT, 128, WBUF_ELEMS], BF16, nreg=NSLOT)
    C.cst = P.sbuf("cst", [128, NCONST], F32)
    P.dma(P.sp, C.cst[:], C.consts[:], r=C.consts.all, w=C.cst.all)

    class V:
        def __init__(self, a, b):
            self.a, self.b = a, b
            self.all = C.cst.all

        def __getitem__(self, k):
            return C.cst.t[:, self.a:self.b][k]
    C.ident = V(C_IDENT, C_IDENT + 128)
    C.ones = V(C_ONES, C_ONES + 128)
    C.V = V
    C.fing = P.sbuf("fing", [128, KC], F32)
    P.dma(P.sp, C.fing[:], C.fing_in[:], r=C.fing_in.all, w=C.fing.all)
    C.modt = P.sbuf("modt", [128, 9, KC, 2], F32)
    C.Gs = P.sbuf("Gs", [128, 3, KC, 2], F32)
    C.SHs = P.sbuf("SHs", [128, 3, KC, 2], F32)
    C.GTs = P.sbuf("GTs", [128, 3, KC, 2], F32)
    cv = P.sbuf("cv", [128, KC, 2], F32)
    C.scb = P.sbuf("scb", [128, KC, 2], BF16)
    P.dma(P.sp, cv[:], C.cvec[:], r=C.cvec.all, w=cv.all)
    P.op(P.act, lambda e: e.activation(out=C.scb[:], in_=cv[:], func=AF.Silu), r=cv.all, w=C.scb.all)

    alloc_scratch(P, C, cfg)
    stage_init(P, C, cfg)
    nl = cfg.DEPTH if n_layers is None else n_layers
    for l in range(nl):
        stage_mod(P, C, cfg, l)
        if "ffn0" in stages:
            stage_ffn(P, C, cfg, l, 0)
        if "mix" in stages:
            stage_mixer(P, C, cfg, l)
        if "ffn1" in stages:
            stage_ffn(P, C, cfg, l, 1)
    stage_final(P, C, cfg)
    P.barrier()
    P.es.close()
    return nc, P


BRANCHES = ("A", "B", "C")


def stage_mixer(P, C, cfg, l):
    stage_proj(P, C, cfg, l)
    if "A" in BRANCHES:
        stage_conv(P, C, cfg, l)
    if "B" in BRANCHES:
        stage_gla(P, C, cfg, l)
    if "C" in BRANCHES:
        stage_na(P, C, cfg, l)
    stage_combine(P, C, cfg, l)


def alloc_scratch(P, C, cfg):
    NB = len(cfg.blocks)
    T, KC = cfg.T, cfg.KCD
    C.PC = P.dram("PC", [3 * cfg.CW // 128, 128, T], F32, nreg=NB)
    C.PQ = P.dram("PQ", [2, cfg.GH, 192, T], F32, nreg=NB)
    C.PV = P.dram("PV", [T, cfg.GV], BF16, nreg=NB)
    C.PG = P.dram("PG", [cfg.GV // 128, 128, T], F32, nreg=NB)
    C.PLR = P.dram("PLR", [2, 16, T], F32, nreg=NB)
    C.PNQ = P.dram("PNQ", [cfg.NH, 128, T], BF16, nreg=NB)
    C.PNK = P.dram("PNK", [cfg.NH, 128, T], BF16, nreg=NB)
    C.PNV = P.dram("PNV", [T, cfg.NW], BF16, nreg=NB)
    C.PGA = P.dram("PGA", [3, KC, 128, T], F32, nreg=NB)
    C.YA = P.dram("YA", [cfg.CW // 128, 128, T], BF16, nreg=NB)
    C.YB = P.dram("YB", [cfg.GV // 128, 128, T], BF16, nreg=NB)
    C.YC = P.dram("YC", [cfg.NW // 128, 128, T], BF16, nreg=NB)
    C.OF = P.dram("OF", [cfg.GV // 128, 128, T], F32, nreg=1)
    C.NABT = P.dram("NABT", [cfg.NH * 15, 4096], F32, nreg=1)


def stage_proj(P, C, cfg, l):
    D, KC = cfg.D, cfg.KCD
    W = C.w_in.t[l]
    wreg = C.w_in.all
    C.wslots = {}
    with contextlib.ExitStack() as st:
        nt_tiles = alloc_norm_tiles(P, st, "pj")
        uT = P.sbuf("pj_uT", [128, KC, 512], BF16, stack=st)
        s32 = [P.sbuf(f"pj_s32_{k}", [128, 512], F32, stack=st) for k in range(4)]
        s16 = [P.sbuf(f"pj_s16_{k}", [128, 512], BF16, stack=st) for k in range(4)]
        rp = [P.sbuf(f"pj_rope{k}", [128, 512], F32, stack=st) for k in range(4)]
        t1 = [P.sbuf(f"pj_t1_{k}", [128, 512], F32, stack=st) for k in range(2)]
        t2 = [P.sbuf(f"pj_t2_{k}", [128, 512], F32, stack=st) for k in range(2)]
        cnt = [0]

        def stg(pool):
            cnt[0] += 1
            return pool[cnt[0] % len(pool)]

        for blk, (t0, nt, kind) in ((b_, cfg.blocks[b_]) for b_ in cfg.border):
            norm_block(P, C, cfg, nt_tiles, 1, t0, nt, kind, uT, blk)

            def ev_fm(dst_fn, func=None, scale=None, dt32=True):
                def evac(ci, coff, m, ps):
                    o = stg(s32 if dt32 else s16)
                    if func is not None:
                        P.op(P.act, lambda e: e.activation(out=o[:m, :nt], in_=ps[:m, :nt], func=func), r=ps.all, w=o.all)
                    elif scale is not None:
                        P.op(P.act, lambda e: e.mul(out=o[:m, :nt], in_=ps[:m, :nt], mul=scale), r=ps.all, w=o.all)
                    else:
                        E = P.ev_eng()
                        if E is P.act:
                            P.op(E, lambda e: e.copy(out=o[:m, :nt], in_=ps[:m, :nt]), r=ps.all, w=o.all)
                        else:
                            P.op(E, lambda e: e.tensor_copy(out=o[:m, :nt], in_=ps[:m, :nt]), r=ps.all, w=o.all)
                    dst, dreg = dst_fn(ci, m)
                    P.dma(P.sp, dst, o[:m, :nt], r=o.all, w=dreg)
                return evac

            def ev_tm(Dst):
                def evac(ti, g0, gc, ps):
                    o = stg(s16)
                    E = P.ev_eng()
                    if E is P.act:
                        P.op(E, lambda e: e.copy(out=o[:, :gc], in_=ps[:, :gc]), r=ps.all, w=o.all)
                    else:
                        P.op(E, lambda e: e.tensor_copy(out=o[:, :gc], in_=ps[:, :gc]), r=ps.all, w=o.all)
                    P.dma(P.sp, Dst[t0 + ti * 128:t0 + (ti + 1) * 128, g0:g0 + gc], o[:, :gc], r=o.all, w=Dst.r(blk))
                return evac

            stream_fm(P, C, W, wreg, D, cfg.off["h"], 3 * cfg.CW, uT, uT.all, nt,
                      ev_fm(lambda ci, m: (C.PC[ci, :m, t0:t0 + nt], C.PC.r(blk))))
            for which, nm in enumerate(("q", "k")):
                if kind == 0:
                    lt0 = t0 - cfg.M
                    for k4, (ti, j0, m) in enumerate(((2 * which, 0, 128), (2 * which, 128, 64),
                                                      (2 * which + 1, 0, 128), (2 * which + 1, 128, 64))):
                        P.dma(P.sp, rp[k4][:m, :nt], C.rope[ti, j0:j0 + m, lt0:lt0 + nt], r=C.rope.all, w=rp[k4].all)
                for hd in range(cfg.GH):
                    col = cfg.off[nm] + hd * 192
                    wt, view = load_w(P, C, W, wreg, 0, D, col, 192)
                    if kind == 0:
                        wr, vr = load_w(P, C, W, wreg, 0, D, col, 192, perm48=True)
                    for cix, (j0, m) in enumerate(((0, 128), (128, 64))):
                        ps = P.next_ps()
                        for kc in range(KC):
                            P.op(P.pe, lambda e: e.matmul(ps[:m, :nt], lhsT=view[:, kc, j0:j0 + m], rhs=uT[:, kc, :nt],
                                                          start=(kc == 0), stop=(kc == KC - 1)),
                                 r=wt.all + uT.all, w=ps.all, sig=(kc == KC - 1))
                        o = stg(s32)
                        if kind == 0:
                            ps2 = P.next_ps()
                            for kc in range(KC):
                                P.op(P.pe, lambda e: e.matmul(ps2[:m, :nt], lhsT=vr[:, kc, j0:j0 + m], rhs=uT[:, kc, :nt],
                                                              start=(kc == 0), stop=(kc == KC - 1)),
                                     r=wr.all + uT.all, w=ps2.all, sig=(kc == KC - 1))
                            a = t1[cix]
                            b_ = t2[cix]
                            cosT, sinT = rp[cix], rp[2 + cix]
                            P.op(P.dve, lambda e: e.tensor_tensor(out=a[:m, :nt], in0=ps[:m, :nt], in1=cosT[:m, :nt], op=ALU.mult),
                                 r=ps.all + cosT.all, w=a.all)
                            P.op(P.dve, lambda e: e.tensor_tensor(out=b_[:m, :nt], in0=ps2[:m, :nt], in1=sinT[:m, :nt], op=ALU.mult),
                                 r=ps2.all + sinT.all, w=b_.all)
                            P.op(P.dve, lambda e: e.tensor_tensor(out=o[:m, :nt], in0=a[:m, :nt], in1=b_[:m, :nt], op=ALU.add),
                                 r=a.all + b_.all, w=o.all)
                        else:
                            sc_ = float(cfg.DK ** -0.5) if which == 0 else 1.0
                            P.op(P.act, lambda e: e.mul(out=o[:m, :nt], in_=ps[:m, :nt], mul=sc_), r=ps.all, w=o.all)
                        P.dma(P.sp, C.PQ[which, hd, j0:j0 + m, t0:t0 + nt], o[:m, :nt], r=o.all, w=C.PQ.r(blk))
            stream_tm(P, C, W, wreg, D, cfg.off["v"], cfg.GV, uT, uT.all, nt, ev_tm(C.PV))
            stream_fm(P, C, W, wreg, D, cfg.off["g"], cfg.GV, uT, uT.all, nt,
                      ev_fm(lambda ci, m: (C.PG[ci, :m, t0:t0 + nt], C.PG.r(blk)), func=AF.Silu))
            stream_fm(P, C, W, wreg, D, cfg.off["lr"], 32, uT, uT.all, nt,
                      ev_fm(lambda ci, m: (C.PLR[ci, :m, t0:t0 + nt], C.PLR.r(blk))), group=32, chunk=16)
            stream_fm(P, C, W, wreg, D, cfg.off["nq"], cfg.NW, uT, uT.all, nt,
                      ev_fm(lambda ci, m: (C.PNQ[ci, :m, t0:t0 + nt], C.PNQ.r(blk)), scale=float(cfg.HD ** -0.5), dt32=False))
            stream_fm(P, C, W, wreg, D, cfg.off["nk"], cfg.NW, uT, uT.all, nt,
                      ev_fm(lambda ci, m: (C.PNK[ci, :m, t0:t0 + nt], C.PNK.r(blk)), dt32=False))
            stream_tm(P, C, W, wreg, D, cfg.off["nv"], cfg.NW, uT, uT.all, nt, ev_tm(C.PNV))
            if not (kind == 1 and l == cfg.DEPTH - 1):
                stream_fm(P, C, W, wreg, D, cfg.off["ga"], 3 * D, uT, uT.all, nt,
                          ev_fm(lambda ci, m: (C.PGA[ci // KC, ci % KC, :m, t0:t0 + nt], C.PGA.r(blk)), func=AF.Sigmoid))
    P.barrier()


def stage_conv(P, C, cfg, l):
    NCH = cfg.CW // 128
    with contextlib.ExitStack() as st:
        cw = P.sbuf("cv_w", [128, NCH, 3], F32, stack=st)
        P.dma(P.sp, cw[:], C.conv_w[l], r=C.conv_w.all, w=cw.all)
        hh = [P.sbuf(f"cv_h{k}", [128, 514], F32, stack=st) for k in range(2)]
        cg = [P.sbuf(f"cv_cg{k}", [128, 514], F32, stack=st) for k in range(2)]
        bg = [P.sbuf(f"cv_bg{k}", [128, 512], F32, stack=st) for k in range(2)]
        zt = [P.sbuf(f"cv_z{k}", [128, 514], F32, stack=st) for k in range(2)]
        oo = [P.sbuf(f"cv_o{k}", [128, 512], F32, stack=st) for k in range(2)]
        yy = [P.sbuf(f"cv_y{k}", [128, 512], BF16, stack=st) for k in range(2)]
        n = 0
        for blk, (t0, nt, kind) in ((b_, cfg.blocks[b_]) for b_ in cfg.border):
            if kind == 1 and l == cfg.DEPTH - 1:
                continue
            s0, s1 = (0, cfg.M) if kind == 1 else (cfg.M, cfg.T)
            lo, hi = max(t0 - 1, s0), min(t0 + nt + 1, s1)
            a, b = lo - (t0 - 1), hi - (t0 - 1)
            rregs = [C.PC.regs[bb] for bb in range(len(cfg.blocks))
                     if cfg.blocks[bb][0] < hi and cfg.blocks[bb][0] + cfg.blocks[bb][1] > lo]
            for c in range(NCH):
                h_, c_, b_, z_, o_, y_ = hh[n % 2], cg[n % 2], bg[n % 2], zt[n % 2], oo[n % 2], yy[n % 2]
                n += 1
                P.dma(P.sp, h_[:, a:b], C.PC[c, :, lo:hi], r=rregs, w=h_.all)
                P.dma(P.sp, c_[:, a:b], C.PC[2 * NCH + c, :, lo:hi], r=rregs, w=c_.all)
                P.dma(P.sp, b_[:, :nt], C.PC[NCH + c, :, t0:t0 + nt], r=rregs, w=b_.all)
                P.op(P.dve, lambda e: e.memset(z_[:, 0:nt + 2], 0.0), w=z_.all)
                P.op(P.dve, lambda e: e.tensor_tensor(out=z_[:, a:b], in0=h_[:, a:b], in1=c_[:, a:b], op=ALU.mult),
                     r=h_.all + c_.all, w=z_.all)
                P.op(P.dve, lambda e: e.tensor_scalar(out=o_[:, :nt], in0=z_[:, 0:nt], scalar1=cw[:, c, 0:1], scalar2=None,
                                                      op0=ALU.mult), r=z_.all + cw.all, w=o_.all)
                for k in (1, 2):
                    P.op(P.dve, lambda e: e.scalar_tensor_tensor(out=o_[:, :nt], in0=z_[:, k:nt + k], scalar=cw[:, c, k:k + 1],
                                                                 in1=o_[:, :nt], op0=ALU.mult, op1=ALU.add),
                         r=z_.all + cw.all + o_.all, w=o_.all)
                P.op(P.dve, lambda e: e.tensor_tensor(out=y_[:, :nt], in0=o_[:, :nt], in1=b_[:, :nt], op=ALU.mult),
                     r=o_.all + b_.all, w=y_.all)
                P.dma(P.sp, C.YA[c, :, t0:t0 + nt], y_[:, :nt], r=y_.all, w=C.YA.r(blk))
    P.barrier()


def stage_combine(P, C, cfg, l):
    D, KC = cfg.D, cfg.KCD
    C.wslots = {}
    brs = []
    if "A" in BRANCHES:
        brs.append((0, C.YA, C.wb_conv, cfg.CW // 128))
    if "B" in BRANCHES:
        brs.append((1, C.YB, C.wb_gla, cfg.GV // 128))
    if "C" in BRANCHES:
        brs.append((2, C.YC, C.wb_na, cfg.NW // 128))
    with contextlib.ExitStack() as st:
        sacc = P.sbuf("cb_sacc", [128, KC, 512], F32, stack=st)
        sbf = P.sbuf("cb_sbf", [128, KC, 512], BF16, stack=st)
        act = P.sbuf("cb_act", [128, 12, 512], BF16, stack=st)
        gt = [P.sbuf(f"cb_g{k}", [128, 512], F32, stack=st) for k in range(2)]
        tm = [P.sbuf(f"cb_t{k}", [128, 512], F32, stack=st) for k in range(2)]
        hres = [P.sbuf(f"cb_hr{k}", [128, 512], F32, stack=st) for k in range(2)]
        hnew = [P.sbuf(f"cb_hn{k}", [128, 512], F32, stack=st) for k in range(2)]
        for blk, (t0, nt, kind) in ((b_, cfg.blocks[b_]) for b_ in cfg.border):
            if kind == 1 and l == cfg.DEPTH - 1:
                continue
            for bi, (X, Y, Wb, nK) in enumerate(brs):
                P.dma(P.sp, act[:, 0:nK, :nt], Y.t.ap()[:, :, t0:t0 + nt].rearrange("k p t -> p k t"), r=Y.r(blk), w=act.all)

                def evac(ci, coff, m, ps, X=X, bi=bi):
                    g = gt[ci % 2]
                    P.dma(P.sp, g[:, :nt], C.PGA[X, ci, :, t0:t0 + nt], r=C.PGA.r(blk), w=g.all)
                    if bi == 0:
                        P.op(P.dve, lambda e: e.tensor_tensor(out=sacc[:, ci, :nt], in0=ps[:, :nt], in1=g[:, :nt], op=ALU.mult),
                             r=ps.all + g.all, w=sacc.all)
                    else:
                        t_ = tm[ci % 2]
                        P.op(P.dve, lambda e: e.tensor_tensor(out=t_[:, :nt], in0=ps[:, :nt], in1=g[:, :nt], op=ALU.mult),
                             r=ps.all + g.all, w=t_.all)
                        P.op(P.dve, lambda e: e.tensor_tensor(out=sacc[:, ci, :nt], in0=sacc[:, ci, :nt], in1=t_[:, :nt], op=ALU.add),
                             r=sacc.all + t_.all, w=sacc.all)
                    if bi == len(brs) - 1:
                        P.op(P.act, lambda e: e.copy(out=sbf[:, ci, :nt], in_=sacc[:, ci, :nt]), r=sacc.all, w=sbf.all)

                stream_fm(P, C, Wb.t[l], Wb.all, nK * 128, 0, D, act, act.all, nt, evac)

            hreg = C.HT.r(blk)

            def evac2(ci, coff, m, ps):
                hr = hres[ci % 2]
                hn = hnew[ci % 2]
                P.dma(P.sp, hr[:, :nt], C.HT[ci, :, t0:t0 + nt], r=hreg, w=hr.all)
                P.op(P.dve, lambda e: e.scalar_tensor_tensor(out=hn[:, :nt], in0=ps[:, :nt], scalar=C.GTs[:, 1, ci, kind:kind + 1],
                                                             in1=hr[:, :nt], op0=ALU.mult, op1=ALU.add),
                     r=ps.all + hr.all + C.GTs.all, w=hn.all)
                P.dma(P.sp, C.HT[ci, :, t0:t0 + nt], hn[:, :nt], r=hn.all, w=hreg)

            stream_fm(P, C, C.w_out.t[l], C.w_out.all, D, 0, D, sbf, sbf.all, nt, evac2)
    P.barrier()


def stage_gla(P, C, cfg, l):
    M, T = cfg.M, cfg.T
    NT = T // 128
    GH = cfg.GH
    nctx = M // 128
    with contextlib.ExitStack() as st:
        sb = lambda n, shp, dt=F32: P.sbuf("gl_" + n, shp, dt, stack=st)
        waug = sb("waug", [17, 2, cfg.GK])
        P.dma(P.sp, waug[:], C.decay_w.t[l].rearrange("d k n -> k d n"), r=C.decay_w.all, w=waug.all)
        ng = sb("ng", [128, cfg.GV // 128])
        P.dma(P.sp, ng[:], C.gla_ng[l], r=C.gla_ng.all, w=ng.all)
        lra = [sb(f"lra{k}", [32, 128]) for k in range(2)]
        for t_ in lra:
            P.op(P.dve, lambda e: e.memset(t_[:], 1.0), w=t_.all)
        e1 = [sb(f"e1_{k}", [128, cfg.GK]) for k in range(2)]
        nla = [sb(f"nla{k}", [128, cfg.GK]) for k in range(2)]
        qA = [sb(f"qA{k}", [128, GH, 128]) for k in range(2)]
        qB = [sb(f"qB{k}", [64, GH, 128]) for k in range(2)]
        kA = [sb(f"kA{k}", [128, GH, 128]) for k in range(2)]
        kB = [sb(f"kB{k}", [64, GH, 128]) for k in range(2)]
        vt = [sb(f"v{k}", [128, cfg.GV], BF16) for k in range(2)]
        eq = [sb(f"eq{k}", [128, 2, 128]) for k in range(2)]
        ek = [sb(f"ek{k}", [128, 2, 128]) for k in range(2)]
        qd = [sb(f"qd{k}", [128, 2, 128], BF16) for k in range(2)]
        ki = [sb(f"ki{k}", [128, 2, 128], BF16) for k in range(2)]
        dk_ = [sb(f"dk{k}", [128, 192]) for k in range(2)]
        kd = [sb(f"kd{k}", [128, 192], BF16) for k in range(2)]
        dec = [sb(f"dec{k}", [128, 4]) for k in range(2)]
        am = [sb(f"am{k}", [128, 128], BF16) for k in range(2)]
        SA = [sb(f"SA{h}", [128, 384]) for h in range(GH)]
        SB = [sb(f"SB{h}", [64, 384]) for h in range(GH)]
        SAb = [sb(f"SAb{h}", [128, 384], BF16) for h in range(GH)]
        SBb = [sb(f"SBb{h}", [64, 384], BF16) for h in range(GH)]
        osum = [sb(f"osum{k}", [128, 3, 128]) for k in range(2)]
        ofl = [sb(f"ofl{k}", [128, 3, 128]) for k in range(2)]
        sq = [sb(f"sq{k}", [128, 3, 128]) for k in range(2)]
        sg = [sb(f"sg{k}", [128, 3, 128]) for k in range(2)]
        rstd = [sb(f"rstd{k}", [128, 128]) for k in range(2)]
        tt = [sb(f"tt{k}", [128, 128]) for k in range(2)]
        yb = [sb(f"yb{k}", [128, 3, 128], BF16) for k in range(2)]
        V = C.V
        cnt = 0
        for dr in range(2):
            TRI = V(C_TRIF, C_TRIF + 128) if dr == 0 else V(C_TRIB, C_TRIB + 128)
            STR = V(C_STRF, C_STRF + 128) if dr == 0 else V(C_STRB, C_STRB + 128)
            IND = V(C_IND, C_IND + 2)
            for h in range(GH):
                P.op(P.dve, lambda e: e.memset(SA[h][:], 0.0), w=SA[h].all)
                P.op(P.dve, lambda e: e.memset(SB[h][:], 0.0), w=SB[h].all)
                P.op(P.dve, lambda e: e.memset(SAb[h][:], 0.0), w=SAb[h].all)
                P.op(P.dve, lambda e: e.memset(SBb[h][:], 0.0), w=SBb[h].all)
            if dr == 0:
                order = list(range(NT))
            else:
                order = list(range(nctx - 1, -1, -1)) + list(range(NT - 1, nctx - 1, -1))
            chunks = (0, 1) if dr == 0 else (1, 0)
            for ti, n in enumerate(order):
                t0 = n * 128
                b2 = ti % 2
                la_, e1_, nl_ = lra[b2], e1[b2], nla[b2]
                P.dma(P.sp, la_[0:16, :], C.PLR[dr, :, t0:t0 + 128], r=C.PLR.all, w=la_.all)
                for hf in range(2):
                    ps = P.next_ps()
                    P.op(P.pe, lambda e: e.matmul(ps[:, :384], lhsT=la_[0:17, :], rhs=waug[0:17, dr, hf * 384:(hf + 1) * 384],
                                                  start=True, stop=True), r=la_.all + waug.all, w=ps.all)
                    P.op(P.act, lambda e: e.activation(out=e1_[:, hf * 384:(hf + 1) * 384], in_=ps[:, :384], func=AF.Exp, scale=-1.0),
                         r=ps.all, w=e1_.all)
                P.op(P.act, lambda e: e.activation(out=nl_[:], in_=e1_[:], func=AF.Ln, bias=1.0, scale=1.0), r=e1_.all, w=nl_.all)
                qa, qb_, ka, kb_, v_ = qA[b2], qB[b2], kA[b2], kB[b2], vt[b2]
                P.dma(P.sp, qa[:], C.PQ.t.ap()[0, :, 0:128, t0:t0 + 128].rearrange("h p t -> p h t"), r=C.PQ.all, w=qa.all)
                P.dma(P.sp, qb_[:], C.PQ.t.ap()[0, :, 128:192, t0:t0 + 128].rearrange("h p t -> p h t"), r=C.PQ.all, w=qb_.all)
                P.dma(P.sp, ka[:], C.PQ.t.ap()[1, :, 0:128, t0:t0 + 128].rearrange("h p t -> p h t"), r=C.PQ.all, w=ka.all)
                P.dma(P.sp, kb_[:], C.PQ.t.ap()[1, :, 128:192, t0:t0 + 128].rearrange("h p t -> p h t"), r=C.PQ.all, w=kb_.all)
                P.dma(P.sp, v_[:], C.PV[t0:t0 + 128, :], r=C.PV.all, w=v_.all)
                for h in range(GH):
                    cnt += 1
                    c2 = cnt % 2
                    d0 = h * 192
                    eq_, ek_, qd_, ki_, dkk, kd_, dec_, am_ = eq[c2], ek[c2], qd[c2], ki[c2], dk_[c2], kd[c2], dec[c2], am[c2]
                    pc = P.next_ps()
                    P.op(P.pe, lambda e: e.matmul(pc[:, 0:128], lhsT=nl_[:, d0:d0 + 128], rhs=TRI[:], start=True, stop=True),
                         r=nl_.all + TRI.all, w=pc.all, sig=False)
                    P.op(P.pe, lambda e: e.matmul(pc[:64, 128:256], lhsT=nl_[:, d0 + 128:d0 + 192], rhs=TRI[:], start=False, stop=True,
                                                  skip_group_check=True), r=nl_.all + TRI.all, w=pc.all)
                    for (pp, cs, ci) in ((128, 0, 0), (64, 128, 1)):
                        P.op(P.act, lambda e: e.activation(out=eq_[:pp, ci, :], in_=pc[:pp, cs:cs + 128], func=AF.Exp, scale=-1.0 / 16),
                             r=pc.all, w=eq_.all)
                        P.op(P.act, lambda e: e.activation(out=ek_[:pp, ci, :], in_=pc[:pp, cs:cs + 128], func=AF.Exp, scale=1.0 / 16),
                             r=pc.all, w=ek_.all)
                    P.op(P.dve, lambda e: e.tensor_tensor(out=qd_[:, 0, :], in0=qa[:, h, :], in1=eq_[:, 0, :], op=ALU.mult),
                         r=qa.all + eq_.all, w=qd_.all)
                    P.op(P.dve, lambda e: e.tensor_tensor(out=qd_[:64, 1, :], in0=qb_[:, h, :], in1=eq_[:64, 1, :], op=ALU.mult),
                         r=qb_.all + eq_.all, w=qd_.all)
                    P.op(P.dve, lambda e: e.tensor_tensor(out=ki_[:, 0, :], in0=ka[:, h, :], in1=ek_[:, 0, :], op=ALU.mult),
                         r=ka.all + ek_.all, w=ki_.all)
                    P.op(P.dve, lambda e: e.tensor_tensor(out=ki_[:64, 1, :], in0=kb_[:, h, :], in1=ek_[:64, 1, :], op=ALU.mult),
                         r=kb_.all + ek_.all, w=ki_.all)
                    pr = P.next_ps()
                    P.op(P.pe, lambda e: e.matmul(pr[:, 0:192], lhsT=STR[:], rhs=nl_[:, d0:d0 + 192], start=True, stop=True),
                         r=nl_.all + STR.all, w=pr.all)
                    P.op(P.act, lambda e: e.activation(out=dkk[:], in_=pr[:, 0:192], func=AF.Exp, scale=-1.0 / 16), r=pr.all, w=dkk.all)
                    pk = P.next_ps()
                    P.op(P.pe, lambda e: e.transpose(pk[:, 0:128], ka[:, h, :], C.ident[:]), r=ka.all + C.ident.all, w=pk.all, sig=False)
                    P.op(P.pe, lambda e: e.transpose(pk[:, 128:192], kb_[:, h, :], C.ident[:64, :64]), r=kb_.all + C.ident.all, w=pk.all)
                    P.op(P.dve, lambda e: e.tensor_tensor(out=kd_[:], in0=pk[:, 0:192], in1=dkk[:], op=ALU.mult),
                         r=pk.all + dkk.all, w=kd_.all)
                    pt = P.next_ps()
                    P.op(P.pe, lambda e: e.matmul(pt[:, 0:2], lhsT=nl_[:, d0:d0 + 128], rhs=IND[:], start=True, stop=True),
                         r=nl_.all + IND.all, w=pt.all, sig=False)
                    P.op(P.pe, lambda e: e.matmul(pt[:64, 2:4], lhsT=nl_[:, d0 + 128:d0 + 192], rhs=IND[:], start=False, stop=True,
                                                  skip_group_check=True), r=nl_.all + IND.all, w=pt.all)
                    P.op(P.act, lambda e: e.activation(out=dec_[:, 0:2], in_=pt[:, 0:2], func=AF.Exp, scale=-1.0 / 16), r=pt.all, w=dec_.all)
                    P.op(P.act, lambda e: e.activation(out=dec_[:64, 2:4], in_=pt[:64, 2:4], func=AF.Exp, scale=-1.0 / 16), r=pt.all, w=dec_.all)
                    pa = P.next_ps()
                    P.op(P.pe, lambda e: e.matmul(pa[:, 0:128], lhsT=ki_[:, 0, :], rhs=qd_[:, 0, :], start=True, stop=False),
                         r=ki_.all + qd_.all, w=pa.all, sig=False)
                    P.op(P.pe, lambda e: e.matmul(pa[:, 0:128], lhsT=ki_[:64, 1, :], rhs=qd_[:64, 1, :], start=False, stop=True),
                         r=ki_.all + qd_.all, w=pa.all)
                    P.op(P.dve, lambda e: e.tensor_tensor(out=am_[:], in0=pa[:, 0:128], in1=TRI[:], op=ALU.mult),
                         r=pa.all + TRI.all, w=am_.all)
                    po = P.next_ps()
                    first = [True]

                    def inter(c):
                        for m in range(3):
                            cs = m * 128 + c * 64
                            P.op(P.pe, lambda e: e.matmul(po[:, cs:cs + 64], lhsT=SAb[h][:, m * 128:(m + 1) * 128],
                                                          rhs=qd_[:, 0, c * 64:(c + 1) * 64], start=False, stop=False, skip_group_check=True),
                                 r=SAb[h].all + qd_.all, w=po.all, sig=False)
                            P.op(P.pe, lambda e: e.matmul(po[:, cs:cs + 64], lhsT=SBb[h][:, m * 128:(m + 1) * 128],
                                                          rhs=qd_[:64, 1, c * 64:(c + 1) * 64], start=False, stop=False, skip_group_check=True),
                                 r=SBb[h].all + qd_.all, w=po.all, sig=(m == 2))

                    def update(c):
                        pva = P.next_ps()
                        pvb = P.next_ps()
                        P.op(P.pe, lambda e: e.matmul(pva[:, 0:384], lhsT=kd_[c * 64:(c + 1) * 64, 0:128],
                                                      rhs=v_[c * 64:(c + 1) * 64, h * 384:(h + 1) * 384], start=True, stop=True),
                             r=kd_.all + v_.all, w=pva.all)
                        P.op(P.pe, lambda e: e.matmul(pvb[:64, 0:384], lhsT=kd_[c * 64:(c + 1) * 64, 128:192],
                                                      rhs=v_[c * 64:(c + 1) * 64, h * 384:(h + 1) * 384], start=True, stop=True),
                             r=kd_.all + v_.all, w=pvb.all)
                        P.op(P.dve, lambda e: e.scalar_tensor_tensor(out=SA[h][:], in0=SA[h][:], scalar=dec_[:, c:c + 1], in1=pva[:, 0:384],
                                                                     op0=ALU.mult, op1=ALU.add), r=SA[h].all + dec_.all + pva.all, w=SA[h].all)
                        P.op(P.dve, lambda e: e.scalar_tensor_tensor(out=SB[h][:], in0=SB[h][:], scalar=dec_[:64, 2 + c:3 + c], in1=pvb[:64, 0:384],
                                                                     op0=ALU.mult, op1=ALU.add), r=SB[h].all + dec_.all + pvb.all, w=SB[h].all)
                        P.op(P.act, lambda e: e.copy(out=SAb[h][:], in_=SA[h][:]), r=SA[h].all, w=SAb[h].all)
                        P.op(P.act, lambda e: e.copy(out=SBb[h][:], in_=SB[h][:]), r=SB[h].all, w=SBb[h].all)

                    for m in range(3):
                        P.op(P.pe, lambda e: e.matmul(po[:, m * 128:(m + 1) * 128], lhsT=v_[:, h * 384 + m * 128:h * 384 + (m + 1) * 128],
                                                      rhs=am_[:], start=(m == 0), stop=False, skip_group_check=True),
                             r=v_.all + am_.all, w=po.all, sig=False)
                    inter(chunks[0])
                    update(chunks[0])
                    inter(chunks[1])
                    update(chunks[1])
                    pov = po[:, 0:384].rearrange("p (m t) -> p m t", t=128)
                    ofd = C.OF.t.ap()[h * 3:(h + 1) * 3, :, t0:t0 + 128].rearrange("m p t -> p m t")
                    if dr == 0:
                        o_ = osum[c2]
                        P.op(P.act, lambda e: e.copy(out=o_[:], in_=pov), r=po.all, w=o_.all)
                        P.dma(P.sp, ofd, o_[:], r=o_.all, w=C.OF.all)
                    else:
                        o_, of_, sq_, sg_, rs_, y_ = osum[c2], ofl[c2], sq[c2], sg[c2], rstd[c2], yb[c2]
                        P.dma(P.sp, of_[:], ofd, r=C.OF.all, w=of_.all)
                        P.dma(P.sp, sg_[:], C.PG.t.ap()[h * 3:(h + 1) * 3, :, t0:t0 + 128].rearrange("m p t -> p m t"), r=C.PG.all, w=sg_.all)
                        P.op(P.dve, lambda e: e.tensor_tensor(out=o_[:], in0=pov, in1=of_[:], op=ALU.add), r=po.all + of_.all, w=o_.all)
                        P.op(P.act, lambda e: e.activation(out=sq_[:], in_=o_[:], func=AF.Square), r=o_.all, w=sq_.all)
                        pss = P.next_ps()
                        for m in range(3):
                            P.op(P.pe, lambda e: e.matmul(pss[:, 0:128], lhsT=C.ones[:], rhs=sq_[:, m, :], start=(m == 0), stop=(m == 2)),
                                 r=sq_.all + C.ones.all, w=pss.all, sig=(m == 2))
                        rstd_from_ss(P, cfg.DV, rs_, pss, 128)
                        for m in range(3):
                            t_ = tt[m % 2]
                            P.op(P.dve, lambda e: e.scalar_tensor_tensor(out=t_[:], in0=o_[:, m, :], scalar=ng[:, h * 3 + m:h * 3 + m + 1],
                                                                         in1=rs_[:], op0=ALU.mult, op1=ALU.mult),
                                 r=o_.all + ng.all + rs_.all, w=t_.all)
                            P.op(P.dve, lambda e: e.tensor_tensor(out=y_[:, m, :], in0=t_[:], in1=sg_[:, m, :], op=ALU.mult),
                                 r=t_.all + sg_.all, w=y_.all)
                        blk = [bb for bb in range(len(cfg.blocks)) if cfg.blocks[bb][0] <= t0 < cfg.blocks[bb][0] + cfg.blocks[bb][1]][0]
                        P.dma(P.sp, C.YB.t.ap()[h * 3:(h + 1) * 3, :, t0:t0 + 128].rearrange("m p t -> p m t"), y_[:], r=y_.all, w=C.YB.r(blk))
    P.barrier()


def na_valid_rows(cfg):
    rows = cfg.ROWS
    kr = min(cfg.KR, rows)
    ws = [min(max(r - cfg.KR // 2, 0), rows - kr) for r in range(rows)]
    return [[r for r in range(rows) if ws[r] <= rp < ws[r] + kr] for rp in range(rows)]


def stage_na(P, C, cfg, l):
    M, T, L = cfg.M, cfg.T, cfg.L
    NT = T // 128
    vq = na_valid_rows(cfg)
    nctx = M // 128
    with contextlib.ExitStack() as st:
        bias = P.sbuf("na_bias", [128, cfg.NH * 15, 64], F32, stack=st)
        rpb = P.sbuf("na_rpb", [31, cfg.NH * 15], F32, stack=st)
        sel = P.sbuf("na_sel", [31, 4096], F32, stack=st)
        bstg = [P.sbuf(f"na_bstg{k}", [128, 512], F32, stack=st) for k in range(2)]
        onesb = P.sbuf("na_ones", [128, 128], BF16, stack=st)
        kT = [P.sbuf(f"na_kT{k}", [128, T], BF16, stack=st) for k in range(2)]
        qT = [P.sbuf(f"na_qT{k}", [128, T], BF16, stack=st) for k in range(2)]
        vv = [P.sbuf(f"na_v{k}", [128, NT, 128], BF16, stack=st) for k in range(2)]
        sb_ = [P.sbuf(f"na_sb{k}", [128, 512], F32, stack=st) for k in range(4)]
        ee = [P.sbuf(f"na_e{k}", [128, 512], BF16, stack=st) for k in range(4)]
        rec = [P.sbuf(f"na_rec{k}", [128, 512], F32, stack=st) for k in range(2)]
        yo = [P.sbuf(f"na_yo{k}", [128, 512], BF16, stack=st) for k in range(2)]
        BT = C.NABT
        P.op(P.dve, lambda e: e.tensor_copy(out=onesb[:], in_=C.ones[:]), r=C.ones.all, w=onesb.all)
        P.dma(P.sp, rpb[:], C.rpbT[l], r=C.rpbT.all, w=rpb.all)
        P.dma(P.sp, sel[:], C.nasel[:], r=C.nasel.all, w=sel.all)
        nrow = cfg.NH * 15
        k = 0
        for m0 in range(0, nrow, 128):
            m = min(128, nrow - m0)
            for n0 in range(0, 4096, 512):
                ps = P.next_ps()
                P.op(P.pe, lambda e: e.matmul(ps[:m, :], lhsT=rpb[:, m0:m0 + m], rhs=sel[:, n0:n0 + 512], start=True, stop=True),
                     r=rpb.all + sel.all, w=ps.all)
                o = bstg[k % 2]
                k += 1
                P.op(P.act, lambda e: e.copy(out=o[:m, :], in_=ps[:m, :]), r=ps.all, w=o.all)
                P.dma(P.sp, BT[m0:m0 + m, n0:n0 + 512], o[:m, :], r=o.all, w=BT.all)
        src = BT.t.ap().rearrange("r (a c) -> a r c", c=64)
        for half in range(2):
            P.dma(P.sp, bias[half * 64:(half + 1) * 64, :, :], src, r=BT.all, w=bias.all)
        cm = C.cst.t[:, C_CMASK:C_CMASK + 64]
        for r_ in range(nrow):
            P.op(P.dve, lambda e: e.tensor_tensor(out=bias[:, r_, :], in0=bias[:, r_, :], in1=cm, op=ALU.add),
                 r=bias.all + C.cst.all, w=bias.all)

        P.ps_rot = 4
        acc = [(P.ps_tiles[4], P.ps_tiles[5]), (P.ps_tiles[6], P.ps_tiles[7])]
        it = 0
        n3 = [0]

        def nxt3():
            n3[0] += 1
            return n3[0] % 4

        for h in range(cfg.NH):
            kt, qt, v = kT[h % 2], qT[h % 2], vv[h % 2]
            P.dma(P.sp, kt[:], C.PNK[h], r=C.PNK.all, w=kt.all)
            P.dma(P.sp, qt[:], C.PNQ[h], r=C.PNQ.all, w=qt.all)
            P.dma(P.sp, v[:], C.PNV.t.ap()[:, h * 128:(h + 1) * 128].rearrange("(n p) d -> p n d", p=128), r=C.PNV.all, w=v.all)
            qblocks = [(M + i * 512, min(512, L - i * 512), i) for i in range((L + 511) // 512)]
            if l != cfg.DEPTH - 1:
                qblocks = [(0, M, -1)] + qblocks
            for (c0, nq, qb) in qblocks:
                po, pd = acc[it % 2]
                it += 1
                items = [("ctx", n) for n in range(nctx)]
                if qb >= 0:
                    r0 = (c0 - M) // 64
                    r1 = r0 + nq // 64
                    for rp in range(cfg.ROWS):
                        qs = [r for r in vq[rp] if r0 <= r < r1]
                        if not qs:
                            continue
                        assert qs == list(range(qs[0], qs[-1] + 1))
                        items.append(("loc", rp, qs[0], qs[-1]))

                def score(item):
                    ps = P.next_ps()
                    if item[0] == "ctx":
                        n = item[1]
                        P.op(P.pe, lambda e: e.matmul(ps[:, :nq], lhsT=kt[:, n * 128:(n + 1) * 128], rhs=qt[:, c0:c0 + nq], start=True, stop=True),
                             r=kt.all + qt.all, w=ps.all)
                    else:
                        _, rp, ra, rb = item
                        N = (rb - ra + 1) * 64
                        tk = M + rp * 64
                        p0 = tk % 128
                        P.op(P.pe, lambda e: e.matmul(ps[p0:p0 + 64, :N], lhsT=kt[:, tk:tk + 64], rhs=qt[:, M + ra * 64:M + ra * 64 + N],
                                                      start=True, stop=True), r=kt.all + qt.all, w=ps.all)
                    return ps

                def consume(idx, item, ps):
                    k3 = nxt3()
                    s_ = sb_[k3]
                    e_ = ee[k3]
                    if item[0] == "ctx":
                        n = item[1]
                        P.op(P.act, lambda e: e.activation(out=e_[:, :nq], in_=ps[:, :nq], func=AF.Exp), r=ps.all, w=e_.all)
                        P.op(P.pe, lambda e: e.matmul(po[:, :nq], lhsT=v[:, n, :], rhs=e_[:, :nq], start=(n == 0), stop=False,
                                                      skip_group_check=True), r=v.all + e_.all, w=po.all)
                        P.op(P.pe, lambda e: e.matmul(pd[:, :nq], lhsT=onesb[:], rhs=e_[:, :nq], start=(n == 0), stop=False,
                                                      skip_group_check=True), r=onesb.all + e_.all, w=pd.all)
                    else:
                        _, rp, ra, rb = item
                        N = (rb - ra + 1) * 64
                        tk = M + rp * 64
                        p0 = tk % 128
                        tn = tk // 128
                        i0 = h * 15 + (7 - rp + ra)
                        bsl = bias[p0:p0 + 64, i0:i0 + (rb - ra + 1), :].rearrange("p a c -> p (a c)")
                        P.op(P.dve, lambda e: e.tensor_tensor(out=s_[p0:p0 + 64, :N], in0=ps[p0:p0 + 64, :N], in1=bsl, op=ALU.add),
                             r=ps.all + bias.all, w=s_.all)
                        P.op(P.act, lambda e: e.activation(out=e_[p0:p0 + 64, :N], in_=s_[p0:p0 + 64, :N], func=AF.Exp), r=s_.all, w=e_.all)
                        cc = (ra - r0) * 64
                        P.op(P.pe, lambda e: e.matmul(po[:, cc:cc + N], lhsT=v[p0:p0 + 64, tn, :], rhs=e_[p0:p0 + 64, :N], start=False, stop=False,
                                                      skip_group_check=True), r=v.all + e_.all, w=po.all)
                        P.op(P.pe, lambda e: e.matmul(pd[:, cc:cc + N], lhsT=onesb[p0:p0 + 64, :], rhs=e_[p0:p0 + 64, :N], start=False, stop=False,
                                                      skip_group_check=True), r=onesb.all + e_.all, w=pd.all)

                AHEAD = 2
                pend = [score(it_) for it_ in items[:AHEAD]]
                for idx, item in enumerate(items):
                    if AHEAD == 0:
                        pend.append(score(item))
                    ps_cur = pend.pop(0)
                    if AHEAD > 0 and idx + AHEAD < len(items):
                        pend.append(score(items[idx + AHEAD]))
                    consume(idx, item, ps_cur)
                rc_ = rec[it % 2]
                y_ = yo[it % 2]
                P.op(P.dve, lambda e: e.reciprocal(out=rc_[:, :nq], in_=pd[:, :nq]), r=pd.all, w=rc_.all)
                P.op(P.dve, lambda e: e.tensor_tensor(out=y_[:, :nq], in0=po[:, :nq], in1=rc_[:, :nq], op=ALU.mult),
                     r=po.all + rc_.all, w=y_.all)
                blks = [bb for bb in range(len(cfg.blocks)) if cfg.blocks[bb][0] == c0]
                P.dma(P.sp, C.YC[h, :, c0:c0 + nq], y_[:, :nq], r=y_.all, w=C.YC.r(blks[0]))
        P.ps_rot = 8
    P.barrier()


def fm_vec(v, kc):
    v = np.asarray(v, np.float32)
    return np.ascontiguousarray(np.swapaxes(v.reshape(v.shape[:-1] + (kc, 128)), -1, -2))


def rope_tables(cfg):
    n = 48
    freq = (10000.0 ** (-np.arange(n, dtype=np.float32) / n)).astype(np.float32)
    pos = np.arange(cfg.L)
    ang_r = (pos // cfg.GW).astype(np.float32)[None, :] * freq[:, None]
    ang_c = (pos % cfg.GW).astype(np.float32)[None, :] * freq[:, None]
    cos = np.concatenate([np.cos(ang_r), np.cos(ang_r), np.cos(ang_c), np.cos(ang_c)], 0).astype(np.float32)
    sin = np.concatenate([-np.sin(ang_r), np.sin(ang_r), -np.sin(ang_c), np.sin(ang_c)], 0).astype(np.float32)
    qs = np.float32(cfg.DK ** -0.5)
    return np.ascontiguousarray(np.stack([cos * qs, sin * qs, cos, sin]).astype(np.float32))


def na_sel():
    sel = np.zeros((31, 64, 64), np.float32)
    for kc_ in range(64):
        for qc_ in range(64):
            d = kc_ - qc_ + 15
            if 0 <= d <= 30:
                sel[d, kc_, qc_] = 1.0
    return sel.reshape(31, 4096)


def shared_inputs(cfg, inp):
    KC = cfg.KCD
    dep = cfg.DEPTH
    sh = {}
    sh["mod_a"] = np.ascontiguousarray(inp["mod_a"], np.float32)
    sh["mod_b"] = np.ascontiguousarray(inp["mod_b"], np.float32)
    sh["mod_bias"] = fm_vec(np.asarray(inp["mod_bias"]).reshape(dep, 9, cfg.D), KC).transpose(0, 2, 1, 3).copy()
    sh["norm_g"] = fm_vec(inp["norm_g"], KC).transpose(0, 2, 1, 3).copy()
    sh["ffn_up"] = np.ascontiguousarray(inp["ffn_up"], np.float32)
    sh["ffn_down"] = np.ascontiguousarray(inp["ffn_down"], np.float32)
    sh["w_in"] = np.ascontiguousarray(inp["w_in"], np.float32)
    cw = np.asarray(inp["conv_w"], np.float32)
    sh["conv_w"] = np.ascontiguousarray(fm_vec(cw, cfg.CW // 128).transpose(0, 2, 3, 1))
    dw = np.asarray(inp["gla_decay_w"], np.float32)
    db = np.asarray(inp["gla_decay_b"], np.float32)[:, :, None, :]
    sh["decay_w"] = np.ascontiguousarray(np.concatenate([dw, db], axis=2))
    sh["gla_ng"] = fm_vec(inp["gla_norm_g"], cfg.GV // 128)
    rp = np.asarray(inp["na_rpb"], np.float32)[:, :, ::-1, :].reshape(dep, cfg.NH * 15, 31)
    sh["rpbT"] = np.ascontiguousarray(rp.transpose(0, 2, 1))
    sh["wb_conv"] = np.ascontiguousarray(inp["w_branch_conv"], np.float32)
    sh["wb_gla"] = np.ascontiguousarray(inp["w_branch_gla"], np.float32)
    sh["wb_na"] = np.ascontiguousarray(inp["w_branch_na"], np.float32)
    sh["w_out"] = np.ascontiguousarray(inp["w_out"], np.float32)
    sh["final_g"] = fm_vec(inp["final_g"], KC)
    sh["consts"] = host_consts()
    sh["rope"] = rope_tables(cfg)
    sh["nasel"] = na_sel()
    return sh


def core_inputs(cfg, inp, sh, b):
    m = dict(sh)
    m["x_in"] = np.ascontiguousarray(inp["x"][b], np.float32)
    m["ctx_in"] = np.ascontiguousarray(inp["ctx"][b], np.float32)
    cv = np.stack([np.asarray(inp["c"][b], np.float32), np.asarray(inp["c_ctx"], np.float32)], 0)
    m["cvec"] = np.ascontiguousarray(fm_vec(cv, cfg.KCD).transpose(1, 2, 0))
    return m


_CACHE = {}


def kernel(**inputs):
    cfg = Cfg()
    if "nc" not in _CACHE:
        _CACHE["nc"] = build(cfg)[0]
    nc = _CACHE["nc"]
    sh = shared_inputs(cfg, inputs)
    nb = inputs["x"].shape[0]
    in_maps = [core_inputs(cfg, inputs, sh, b) for b in range(nb)]
    res = run_bass_kernel_spmd(nc, in_maps, core_ids=list(range(nb)))
    return np.stack([np.asarray(r["out"], np.float32) for r in res.results], 0)
```
